# Optimizing a Trainium2 kernel written in Bass

```python
import jax, jax.numpy as jnp
from jax import lax
import numpy as np

D_MODEL = 2048
BATCH = 1
SEQ = 16384
DEPTH = 4
DEC_BATCH = 16
DEC_SEQ = 2048
PAST_LEN = 128

GRID_W = 64
N_MIXERS = 3
EXPAND = 2
D_INNER = EXPAND * D_MODEL
RMS_EPS = 1e-6

FNET_GROUPS = 8
FNET_GROUP_W = D_INNER // FNET_GROUPS

NAT_HEADS = 32
NAT_HEAD_DIM = D_INNER // NAT_HEADS
NAT_WIN_H = 8
NAT_WIN_W = 16
NAT_COL_BLOCK = 16
NAT_KEY_COLS = 32
N_COL_BLOCKS = GRID_W // NAT_COL_BLOCK

GLA_HEADS = 4
GLA_KEY_DIM = D_MODEL // 2
GLA_HEAD_K = GLA_KEY_DIM // GLA_HEADS
GLA_HEAD_V = D_INNER // GLA_HEADS
GLA_GATE_RANK = 16
GLA_GATE_TEMP = 16.0
GLA_CHUNK = 64

N_FNET_LAYERS = (DEPTH + 2) // 3
N_NAT_LAYERS = (DEPTH + 1) // 3
N_GLA_LAYERS = DEPTH // 3

kernel_name = "hybrid_fnet_nat_gla_encoder"


def _rmsnorm(x, g):
    xf = x.astype(jnp.float32)
    y = xf * lax.rsqrt(jnp.mean(xf * xf, axis=-1, keepdims=True) + RMS_EPS)
    return (y * g.astype(jnp.float32)).astype(x.dtype)


def _fnet_mixer(h, w_in, w_out):
    B, S, _ = h.shape
    u, z = jnp.split(h @ w_in, 2, axis=-1)
    ug = u.astype(jnp.float32).reshape(B, S, FNET_GROUPS, FNET_GROUP_W)
    mixed = jnp.fft.fft2(ug, axes=(1, 3), norm="ortho").real
    mixed = mixed.astype(h.dtype).reshape(B, S, D_INNER)
    return (mixed * jax.nn.silu(z)) @ w_out


def _nat_col_tables():
    q_cols = np.arange(GRID_W)
    win_start = np.clip(q_cols - NAT_WIN_W // 2, 0, GRID_W - NAT_WIN_W)
    band_start = np.clip(np.arange(N_COL_BLOCKS) * NAT_COL_BLOCK - NAT_WIN_W // 2,
                         0, GRID_W - NAT_KEY_COLS)
    qc = q_cols.reshape(N_COL_BLOCKS, NAT_COL_BLOCK)
    ws = win_start.reshape(N_COL_BLOCKS, NAT_COL_BLOCK)
    kc = band_start[:, None] + np.arange(NAT_KEY_COLS)[None]
    valid = (kc[:, None, :] >= ws[:, :, None]) & (kc[:, None, :] < ws[:, :, None] + NAT_WIN_W)
    rel = np.clip(kc[:, None, :] - qc[:, :, None], -(NAT_WIN_W - 1), NAT_WIN_W - 1) + NAT_WIN_W - 1
    return band_start, valid, rel


def _nat_mixer(h, w_in, rpb, w_out):
    B, S, _ = h.shape
    rows = S // GRID_W
    kh = min(NAT_WIN_H, rows)
    q, k, v, z = jnp.split(h @ w_in, 4, axis=-1)
    grid = lambda t: t.reshape(B, rows, GRID_W, NAT_HEADS, NAT_HEAD_DIM)
    q = grid(q) * (NAT_HEAD_DIM ** -0.5)
    k = grid(k)
    v = grid(v)
    band_start, valid, rel = _nat_col_tables()
    valid = jnp.asarray(valid)
    rpb_cols = rpb[:, :, jnp.asarray(rel)]

    def row_step(r):
        rs = jnp.clip(r - kh // 2, 0, rows - kh)
        k_band = lax.dynamic_slice_in_dim(k, rs, kh, axis=1)
        v_band = lax.dynamic_slice_in_dim(v, rs, kh, axis=1)
        q_row = lax.dynamic_index_in_dim(q, r, axis=1, keepdims=False)
        row_off = rs + jnp.arange(kh) - r + NAT_WIN_H - 1
        bias = rpb_cols[:, row_off]
        outs = []
        for cb in range(N_COL_BLOCKS):
            c0 = int(band_start[cb])
            kb = k_band[:, :, c0:c0 + NAT_KEY_COLS]
            vb = v_band[:, :, c0:c0 + NAT_KEY_COLS]
            qb = q_row[:, cb * NAT_COL_BLOCK:(cb + 1) * NAT_COL_BLOCK]
            s = jnp.einsum('bqhd,brkhd->bhqrk', qb, kb).astype(jnp.float32)
            s = s + jnp.transpose(bias[:, :, cb], (0, 2, 1, 3)).astype(jnp.float32)[None]
            s = jnp.where(valid[cb][None, None, :, None, :], s, -1e30)
            p = jax.nn.softmax(s.reshape(B, NAT_HEADS, NAT_COL_BLOCK, kh * NAT_KEY_COLS), axis=-1)
            p = p.reshape(B, NAT_HEADS, NAT_COL_BLOCK, kh, NAT_KEY_COLS).astype(vb.dtype)
            outs.append(jnp.einsum('bhqrk,brkhd->bqhd', p, vb))
        return jnp.concatenate(outs, axis=1)

    o = lax.map(row_step, jnp.arange(rows))
    o = jnp.transpose(o, (1, 0, 2, 3, 4)).reshape(B, S, D_INNER)
    return (o * jax.nn.silu(z)) @ w_out


def _gla_direction(q, k, v, log_a, strict):
    B, S, H, dk = q.shape
    dv = v.shape[-1]
    C = GLA_CHUNK
    n = S // C
    chunk = lambda t: t.reshape(B, n, C, H, t.shape[-1])
    q, k, v, log_a = chunk(q), chunk(k), chunk(v), chunk(log_a)
    b = jnp.cumsum(log_a, axis=2)
    b_last = b[:, :, -1]
    b_mid = b[:, :, C // 2 - 1:C // 2]
    scores = jnp.einsum('bnihk,bnjhk->bnhij', q * jnp.exp(b - b_mid), k * jnp.exp(b_mid - b))
    mask = jnp.asarray(np.tril(np.ones((C, C), dtype=bool), k=-1 if strict else 0))
    scores = jnp.where(mask, scores, 0.0)
    o_intra = jnp.einsum('bnhij,bnjhv->bnihv', scores, v)
    q_in = q * jnp.exp(b)
    k_out = k * jnp.exp(b_last[:, :, None] - b)

    def step(state, xs):
        q_c, k_c, v_c, bl = xs
        o = jnp.einsum('bihk,bhkv->bihv', q_c, state)
        state = jnp.exp(bl)[..., None] * state + jnp.einsum('bjhk,bjhv->bhkv', k_c, v_c)
        return state, o

    state0 = jnp.zeros((B, H, dk, dv), jnp.float32)
    xs = (jnp.moveaxis(q_in, 1, 0), jnp.moveaxis(k_out, 1, 0), jnp.moveaxis(v, 1, 0), jnp.moveaxis(b_last, 1, 0))
    _, o_inter = lax.scan(step, state0, xs)
    return (o_intra + jnp.moveaxis(o_inter, 0, 1)).reshape(B, S, H, dv)


def _gla_mixer(h, w_in, wa1_f, wa2_f, ba_f, wa1_b, wa2_b, ba_b, g_norm, w_out):
    B, S, _ = h.shape
    q, k, v, z = jnp.split(h @ w_in, [GLA_KEY_DIM, 2 * GLA_KEY_DIM, 2 * GLA_KEY_DIM + D_INNER], axis=-1)
    heads = lambda t, d: t.astype(jnp.float32).reshape(B, S, GLA_HEADS, d)
    q = heads(q, GLA_HEAD_K) * (GLA_HEAD_K ** -0.5)
    k = heads(k, GLA_HEAD_K)
    v = heads(v, GLA_HEAD_V)

    def log_gate(wa1, wa2, ba):
        pre = ((h @ wa1) @ wa2 + ba).astype(jnp.float32)
        return heads(jax.nn.log_sigmoid(pre) / GLA_GATE_TEMP, GLA_HEAD_K)

    rev = lambda t: jnp.flip(t, axis=1)
    o_f = _gla_direction(q, k, v, log_gate(wa1_f, wa2_f, ba_f), strict=False)
    o_b = rev(_gla_direction(rev(q), rev(k), rev(v), rev(log_gate(wa1_b, wa2_b, ba_b)), strict=True))
    o = o_f + o_b
    o = o * lax.rsqrt(jnp.mean(o * o, axis=-1, keepdims=True) + RMS_EPS) * g_norm.astype(jnp.float32)
    o = o.reshape(B, S, D_INNER).astype(h.dtype)
    return (o * jax.nn.silu(z)) @ w_out


def _trunk(x, norm_pre_g, norm_post_g, fnet_w_in, fnet_w_out, nat_w_in, nat_rpb, nat_w_out,
           gla_w_in, gla_wa1_f, gla_wa2_f, gla_ba_f, gla_wa1_b, gla_wa2_b, gla_ba_b, gla_g_norm, gla_w_out):
    for i in range(DEPTH):
        h = _rmsnorm(x, norm_pre_g[i])
        m, j = i % N_MIXERS, i // N_MIXERS
        if m == 0:
            y = _fnet_mixer(h, fnet_w_in[j], fnet_w_out[j])
        elif m == 1:
            y = _nat_mixer(h, nat_w_in[j], nat_rpb[j], nat_w_out[j])
        else:
            y = _gla_mixer(h, gla_w_in[j], gla_wa1_f[j], gla_wa2_f[j], gla_ba_f[j],
                           gla_wa1_b[j], gla_wa2_b[j], gla_ba_b[j], gla_g_norm[j], gla_w_out[j])
        x = x + _rmsnorm(y, norm_post_g[i])
    return x


def setup_inputs(seed: int = 0) -> dict:
    key = jax.random.key(seed)
    ks = jax.random.split(key, 20)
    nrm = lambda k, shape, scale: jax.random.normal(k, shape, jnp.float32) * scale
    D, E = D_MODEL, D_INNER
    return {
        "x_prompt": nrm(ks[0], (BATCH, SEQ, D), 1.0),
        "x_sample": nrm(ks[1], (DEC_BATCH, DEC_SEQ, D), 1.0),
        "norm_pre_g": 1.0 + nrm(ks[2], (DEPTH, D), 0.02),
        "norm_post_g": 1.0 + nrm(ks[3], (DEPTH, D), 0.02),
        "fnet_w_in": nrm(ks[4], (N_FNET_LAYERS, D, 2 * E), D ** -0.5),
        "fnet_w_out": nrm(ks[5], (N_FNET_LAYERS, E, D), E ** -0.5),
        "nat_w_in": nrm(ks[6], (N_NAT_LAYERS, D, 4 * E), D ** -0.5),
        "nat_rpb": nrm(ks[7], (N_NAT_LAYERS, NAT_HEADS, 2 * NAT_WIN_H - 1, 2 * NAT_WIN_W - 1), 0.1),
        "nat_w_out": nrm(ks[8], (N_NAT_LAYERS, E, D), E ** -0.5),
        "gla_w_in": nrm(ks[9], (N_GLA_LAYERS, D, 2 * GLA_KEY_DIM + 2 * E), D ** -0.5),
        "gla_wa1_f": nrm(ks[10], (N_GLA_LAYERS, D, GLA_GATE_RANK), D ** -0.5),
        "gla_wa2_f": nrm(ks[11], (N_GLA_LAYERS, GLA_GATE_RANK, GLA_KEY_DIM), GLA_GATE_RANK ** -0.5),
        "gla_ba_f": nrm(ks[12], (N_GLA_LAYERS, GLA_KEY_DIM), 0.1),
        "gla_wa1_b": nrm(ks[13], (N_GLA_LAYERS, D, GLA_GATE_RANK), D ** -0.5),
        "gla_wa2_b": nrm(ks[14], (N_GLA_LAYERS, GLA_GATE_RANK, GLA_KEY_DIM), GLA_GATE_RANK ** -0.5),
        "gla_ba_b": nrm(ks[15], (N_GLA_LAYERS, GLA_KEY_DIM), 0.1),
        "gla_g_norm": 1.0 + nrm(ks[16], (N_GLA_LAYERS, GLA_HEAD_V), 0.02),
        "gla_w_out": nrm(ks[17], (N_GLA_LAYERS, E, D), E ** -0.5),
    }


def reference(x_prompt, x_sample, norm_pre_g, norm_post_g, fnet_w_in, fnet_w_out, nat_w_in, nat_rpb,
              nat_w_out, gla_w_in, gla_wa1_f, gla_wa2_f, gla_ba_f, gla_wa1_b, gla_wa2_b, gla_ba_b,
              gla_g_norm, gla_w_out):
    y_prompt = _trunk(x_prompt, norm_pre_g, norm_post_g, fnet_w_in, fnet_w_out, nat_w_in, nat_rpb, nat_w_out,
                      gla_w_in, gla_wa1_f, gla_wa2_f, gla_ba_f, gla_wa1_b, gla_wa2_b, gla_ba_b, gla_g_norm, gla_w_out)
    y_sample = _trunk(x_sample, norm_pre_g, norm_post_g, fnet_w_in, fnet_w_out, nat_w_in, nat_rpb, nat_w_out,
                      gla_w_in, gla_wa1_f, gla_wa2_f, gla_ba_f, gla_wa1_b, gla_wa2_b, gla_ba_b, gla_g_norm, gla_w_out)
    return (y_prompt, y_sample)
```

```python
import contextlib
import numpy as np
import ml_dtypes
import concourse.bass as bass
import concourse.mybir as mybir
from concourse.bass_utils import run_bass_kernel_spmd

F32 = mybir.dt.float32
BF16 = mybir.dt.bfloat16
AF = mybir.ActivationFunctionType
ALU = mybir.AluOpType
AX = mybir.AxisListType
NPBF = ml_dtypes.bfloat16

D = 2048
E = 4096
EPS = 1e-6
GRID_W = 64
LAYERS = [("fnet", 0), ("nat", 0), ("gla", 0), ("fnet", 1)]

ENGS = ["pe", "act", "dve", "pool", "sp"]
SIG_CHUNK = 30000
DMA_SLOTS = 8
DMA_CHUNK = 1800


class Op:
    __slots__ = ("eng", "fn", "deps", "dma", "pos", "sig", "semidx", "slot", "dcount", "waits")

    def __init__(self):
        self.sig = False
        self.dma = False
        self.waits = ()


class Sched:
    def __init__(self):
        self.ops = []
        self.last_w = {}
        self.rd_eng = {}
        self.rd_dma = {}
        self.streams = {e: [] for e in ENGS}
        self.ndma = {e: 0 for e in ENGS}
        self.slot_last = {}
        self.last_compute = {}
        self.pending = {}

    def barrier(self):
        deps = list(self.last_compute.values()) + list(self.slot_last.values())
        self.pending = {e: list(deps) for e in ENGS}
        self.last_w = {}
        self.rd_eng = {}
        self.rd_dma = {}

    def op(self, eng, fn, r=(), w=(), dma=False):
        o = Op()
        o.eng = eng
        o.fn = fn
        o.dma = dma
        deps = []
        pb = self.pending.pop(eng, None)
        if pb:
            deps.extend(pb)
        for x in r:
            lw = self.last_w.get(x)
            if lw is not None:
                deps.append(lw)
        for x in w:
            lw = self.last_w.get(x)
            if lw is not None:
                deps.append(lw)
            de = self.rd_eng.get(x)
            if de:
                deps.extend(de.values())
            dd = self.rd_dma.get(x)
            if dd:
                deps.extend(dd)
        if dma:
            j = self.ndma[eng]
            self.ndma[eng] = j + 1
            slot = j % DMA_SLOTS
            o.slot = slot
            prev = self.slot_last.get((eng, slot))
            o.dcount = (prev.dcount + 1) if prev is not None else 1
            if prev is not None:
                deps.append(prev)
            self.slot_last[(eng, slot)] = o
        else:
            self.last_compute[eng] = o
        o.deps = deps
        for x in w:
            self.last_w[x] = o
            self.rd_eng[x] = {}
            self.rd_dma[x] = []
        for x in r:
            if dma:
                self.rd_dma.setdefault(x, []).append(o)
            else:
                self.rd_eng.setdefault(x, {})[eng] = o
        o.pos = len(self.streams[eng])
        self.streams[eng].append(o)
        self.ops.append(o)
        return o

    def pe(self, fn, r=(), w=()):
        return self.op("pe", fn, r, w)

    def act(self, fn, r=(), w=()):
        return self.op("act", fn, r, w)

    def dve(self, fn, r=(), w=()):
        return self.op("dve", fn, r, w)

    def dma(self, eng, fn, r=(), w=()):
        return self.op(eng, fn, r, w, dma=True)

    def plan(self):
        seen = {e: {} for e in ENGS}
        for o in self.ops:
            waits = []
            se = seen[o.eng]
            for d in o.deps:
                if d.dma:
                    key = ("d", d.eng, d.slot)
                    need = d.dcount
                else:
                    if d.eng == "pe" and o.eng == "pe" and not o.dma:
                        continue
                    key = ("e", d.eng)
                    need = d.pos
                if se.get(key, -1) >= need:
                    continue
                se[key] = need
                if not d.dma:
                    d.sig = True
                waits.append(d)
            o.waits = waits
            o.deps = None
        self.nsig = {}
        for e in ENGS:
            c = 0
            for o in self.streams[e]:
                if o.sig:
                    o.semidx = c
                    c += 1
            self.nsig[e] = c

    def emit(self, nc, stack):
        self.plan()
        esem = {}
        for e in ENGS:
            n = (self.nsig[e] + SIG_CHUNK - 1) // SIG_CHUNK
            esem[e] = [stack.enter_context(nc.semaphore(f"s_{e}_{i}")) for i in range(max(n, 1))]
        dsem = {}
        for (e, slot), last in self.slot_last.items():
            n = (last.dcount + DMA_CHUNK - 1) // DMA_CHUNK
            dsem[(e, slot)] = [stack.enter_context(nc.semaphore(f"d_{e}_{slot}_{i}")) for i in range(n)]

        def sem_of(d):
            if d.dma:
                k = (d.dcount - 1) // DMA_CHUNK
                return dsem[(d.eng, d.slot)][k], 16 * (d.dcount - k * DMA_CHUNK)
            k = d.semidx // SIG_CHUNK
            return esem[d.eng][k], d.semidx - k * SIG_CHUNK + 1

        def run(engobj, ename):
            for o in self.streams[ename]:
                for d in o.waits:
                    s, v = sem_of(d)
                    engobj.wait_ge(s, v)
                ins = o.fn(engobj)
                if o.dma:
                    s, _ = sem_of(o)
                    ins.then_inc(s, 16)
                elif o.sig:
                    s, _ = sem_of(o)
                    ins.then_inc(s, 1)
                o.fn = None
            for (e, slot), last in self.slot_last.items():
                if e == ename:
                    s, v = sem_of(last)
                    engobj.wait_ge(s, v)

        block = stack.enter_context(nc.Block())

        @block.tensor
        def _(e):
            run(e, "pe")

        @block.scalar
        def _(e):
            run(e, "act")

        @block.vector
        def _(e):
            run(e, "dve")

        @block.gpsimd
        def _(e):
            run(e, "pool")

        @block.sync
        def _(e):
            run(e, "sp")


def fnet_consts(S_len):
    R = S_len // 128
    c = np.arange(512)
    ang = 2 * np.pi * np.outer(c, c) / 512.0
    fc = np.concatenate([np.cos(ang), -np.sin(ang)], axis=1) / np.sqrt(512.0)
    fc = fc.reshape(4, 128, 1024).transpose(1, 0, 2)
    n2 = np.arange(128)[:, None, None]
    n1 = np.arange(R)[None, :, None]
    k1 = np.arange(R)[None, None, :]
    th = 2 * np.pi * ((k1 * (128 * n1 + n2)) % S_len) / float(S_len)
    m1 = np.stack([np.cos(th), np.sin(th), -np.sin(th)], axis=2) / np.sqrt(float(R))
    m1 = m1.transpose(1, 0, 2, 3)
    a = np.arange(128)
    ph = 2 * np.pi * np.outer(a, a) / 128.0
    m2 = np.stack([np.cos(ph), np.sin(ph)], axis=1) / np.sqrt(128.0)
    return fc.astype(NPBF), np.ascontiguousarray(m1).astype(NPBF), m2.astype(NPBF)


NAT_PAT = {0: (0, 0), 2: (0, 0), 4: (0, 1), 5: (1, 1), 7: (1, 1)}
NAT_PIDX = {0: 0, 2: 1, 4: 2, 5: 3, 7: 4}


def nat_bias_tables(rpb):
    out = np.full((5, 128, 32, 576), -30000.0, np.float32)
    c = np.arange(64)
    ws = np.clip(c - 8, 0, 48)
    kc = np.arange(64)
    valid_c = (kc[None, :] >= ws[:, None]) & (kc[None, :] < ws[:, None] + 16)
    colidx = np.clip(kc[None, :] - c[:, None], -15, 15) + 15
    for d, (r0, r1) in NAT_PAT.items():
        p = NAT_PIDX[d]
        for rr, rs_rel in enumerate((r0, r1)):
            for kr in range(9):
                if not (rs_rel <= kr < rs_rel + 8):
                    continue
                ridx = kr - d - rr + 7
                vals = rpb[:, ridx, :][:, colidx]
                blk = out[p, rr * 64:(rr + 1) * 64, :, kr * 64:(kr + 1) * 64]
                blk[:] = np.where(valid_c[:, None, :], vals.transpose(1, 0, 2), -30000.0)
    return out.astype(NPBF)


def gla_consts():
    j = np.arange(128)[:, None]
    i = np.arange(128)[None, :]
    sc = -1.0 / 16.0
    f_cum = (j <= i).astype(np.float32)
    f_mid = (j <= 63).astype(np.float32) * np.ones((1, 128), np.float32)
    f_rest = (j > i).astype(np.float32)
    b_cum = (j >= i).astype(np.float32)
    b_mid = (j >= 64).astype(np.float32) * np.ones((1, 128), np.float32)
    b_rest = (j < i).astype(np.float32)
    mats = np.stack([(f_cum - f_mid) * sc, f_cum * sc, f_rest * sc,
                     (b_cum - b_mid) * sc, b_cum * sc, b_rest * sc], axis=1).astype(np.float32)
    masks = np.stack([(j <= i), (j > i)], axis=1).astype(np.float32)
    return mats, masks


MAX_PHASE = 0
NATDBG = 9


class StopBuild(Exception):
    pass


class K:
    def __init__(self, units, layers, arena_elems=100352):
        self.nc = bass.Bass("TRN2", target_bir_lowering=False)
        self.S = Sched()
        self.units = units
        self.layers = layers
        self.arena_elems = arena_elems
        self.uid = 0

    def setup_mem(self, st):
        nc = self.nc
        self.arena = st.enter_context(nc.sbuf_tensor("arena", [128, self.arena_elems], BF16))
        self.ps = [st.enter_context(nc.psum_tensor(f"ps{i}", [128, 512], F32)) for i in range(8)]
        self.off = 0
        self.ident = self.sb([128, 128], BF16)
        self.eps = self.sb([128, 1], F32)
        self.one = self.sb([128, 1], F32)
        self.persist_end = self.off

    def sb(self, shape, dt):
        n = int(np.prod(shape[1:]))
        nb = n * (4 if dt == F32 else 2)
        off = (self.off + 63) // 64 * 64
        assert off + nb <= self.arena_elems * 2, f"arena overflow {off + nb}"
        v = self.arena[0:shape[0], off // 2:(off + nb) // 2]
        if dt == F32:
            v = v.bitcast(F32)
        if len(shape) == 3:
            v = v.rearrange("p (a b) -> p a b", a=shape[1])
        self.off = off + nb
        return v

    def psb(self, i):
        return self.ps[i][:].bitcast(BF16)

    def phase(self):
        self.nphase = getattr(self, "nphase", 0) + 1
        if MAX_PHASE and self.nphase > MAX_PHASE:
            raise StopBuild()
        self.S.barrier()
        self.off = self.persist_end

    def name(self, s):
        self.uid += 1
        return f"{s}#{self.uid}"

    def MM(self, out, lhsT, rhs, start, stop, r, w):
        self.S.pe(lambda e: e.matmul(out, lhsT=lhsT, rhs=rhs, start=start, stop=stop), r=r, w=w)

    def TR(self, out, in_, r, w, ident=None):
        idn = self.ident if ident is None else ident
        self.S.pe(lambda e: e.transpose(out=out, in_=in_, identity=idn), r=r, w=w)

    def ACT(self, out, in_, func, r, w, **kw):
        self.S.act(lambda e: e.activation(out=out, in_=in_, func=func, **kw), r=r, w=w)

    def LD(self, out, in_, r, w, eng="sp"):
        self.S.dma(eng, lambda e: e.dma_start(out=out, in_=in_), r=r, w=w)

    def ST(self, out, in_, r, w, eng="pool"):
        self.S.dma(eng, lambda e: e.dma_start(out=out, in_=in_), r=r, w=w)

    def evac(self, idx, out, in_, r, w, scale=1.0):
        if idx % 2 == 0:
            self.S.act(lambda e: e.activation(out=out, in_=in_, func=AF.Copy, scale=float(scale)), r=r, w=w)
        else:
            if scale == 1.0:
                self.S.dve(lambda e: e.tensor_copy(out=out, in_=in_), r=r, w=w)
            else:
                self.S.dve(lambda e: e.tensor_scalar_mul(out=out, in0=in_, scalar1=float(scale)), r=r, w=w)

    def phaseA(self, x_src, S_len, g_row, fm, tm, gate=None):
        k = self
        S = self.S
        k.phase()
        MT = min(2048, S_len)
        NTT = MT // 128
        NTB = MT // 512
        g_sb = k.sb([128, D], F32)
        xt = [k.sb([128, D], F32) for _ in range(2)]
        junk = k.sb([128, D], BF16)
        hb = [k.sb([128, D], BF16) for _ in range(2)]
        ssq = k.sb([128, 32], F32)
        rstd = k.sb([128, 32], F32)
        hT = k.sb([128, 16, MT], BF16)
        wfm = [k.sb([128, 16, 128], BF16) for _ in range(3)]
        wtm = [k.sb([128, 16, 512], BF16) for _ in range(2)]
        stf = [k.sb([128, MT], BF16) for _ in range(2)]
        stt = [k.sb([128, 512], BF16) for _ in range(4)]
        k.LD(g_sb, g_row.partition_broadcast(128), r=[], w=["g_sb"])
        if gate is not None:
            wa1 = [k.sb([128, 16, 16], BF16) for _ in range(2)]
            wa2 = [k.sb([16, 1024], BF16) for _ in range(2)]
            bab = [k.sb([128, 1024], F32) for _ in range(2)]
            t1T = [k.sb([16, MT], BF16) for _ in range(2)]
            pre = [k.sb([128, 1024], F32) for _ in range(2)]
            for d in range(2):
                k.LD(wa1[d], gate["wa1"][d].rearrange("(kc p) r -> p kc r", p=128), r=[], w=[f"wa1{d}"], eng="pool")
                k.LD(wa2[d], gate["wa2"][d], r=[], w=[f"wa2{d}"], eng="pool")
                k.LD(bab[d], gate["ba"][d].partition_broadcast(128), r=[], w=[f"bab{d}"])
        hT_res = [f"hT{t}{h}" for t in range(NTT) for h in "ab"]
        cnt = 0
        wi = 0
        for mt in range(S_len // MT):
            for t in range(NTT):
                b = t % 2
                col = t % 32
                row0 = mt * MT + t * 128
                k.LD(xt[b], x_src[row0:row0 + 128, :], r=[], w=[f"xt{b}"])
                k.ACT(junk, xt[b], AF.Square, r=[f"xt{b}"], w=["junk", f"ssq{col}"], accum_out=ssq[:, col:col + 1])
                k.ACT(rstd[:, col:col + 1], ssq[:, col:col + 1], AF.Sqrt, r=[f"ssq{col}"], w=[f"rstd{col}"],
                      bias=k.eps[:, 0:1], scale=1.0 / D)
                S.dve(lambda e, o=rstd[:, col:col + 1]: e.reciprocal(out=o, in_=o), r=[f"rstd{col}"], w=[f"rstd{col}"])
                S.dve(lambda e, o=hb[b], i0=xt[b], sc=rstd[:, col:col + 1]: e.scalar_tensor_tensor(
                    out=o, in0=i0, scalar=sc, in1=g_sb, op0=ALU.mult, op1=ALU.mult),
                    r=[f"xt{b}", f"rstd{col}", "g_sb"], w=[f"hb{b}"])
                for kc in range(16):
                    k.TR(k.psb(kc // 8)[:, (kc % 8) * 128:(kc % 8 + 1) * 128], hb[b][:, kc * 128:(kc + 1) * 128],
                         r=[f"hb{b}"], w=[f"ptr{kc // 8}"])
                S.act(lambda e, o=hT[:, 0:8, t * 128:(t + 1) * 128], i=k.psb(0).rearrange("p (a b) -> p a b", a=8):
                      e.copy(out=o, in_=i), r=["ptr0"], w=[f"hT{t}a"])
                S.dve(lambda e, o=hT[:, 8:16, t * 128:(t + 1) * 128], i=k.psb(1).rearrange("p (a b) -> p a b", a=8):
                      e.tensor_copy(out=o, in_=i), r=["ptr1"], w=[f"hT{t}b"])
            for (w_ap, dst) in fm:
                n = w_ap.shape[1]
                wv = w_ap.rearrange("(kc p) n -> p kc n", p=128)
                for dtile in range(n // 128):
                    wb = wi % 3
                    wi += 1
                    k.LD(wfm[wb], wv[:, :, dtile * 128:(dtile + 1) * 128], r=[], w=[f"wfm{wb}"], eng="pool")
                    sf = cnt % 2
                    for tb in range(NTB):
                        pb = 2 + cnt % 6
                        for kc in range(16):
                            k.MM(k.ps[pb][:], wfm[wb][:, kc, :], hT[:, kc, tb * 512:(tb + 1) * 512], kc == 0, kc == 15,
                                 r=[f"wfm{wb}"] + hT_res[tb * 8:(tb + 1) * 8], w=[f"ps{pb}"])
                        k.evac(cnt, stf[sf][:, tb * 512:(tb + 1) * 512], k.ps[pb][:], r=[f"ps{pb}"], w=[f"stf{sf}_{tb}"])
                        cnt += 1
                    k.ST(dst[dtile * 128:(dtile + 1) * 128, mt * MT:(mt + 1) * MT], stf[sf],
                         r=[f"stf{sf}_{tb}" for tb in range(NTB)], w=[k.name("dfm")])
                    cnt += (NTB % 2 == 0)
            for (w_ap, dst, scale) in tm:
                n = w_ap.shape[1]
                wv = w_ap.rearrange("(kc p) n -> p kc n", p=128)
                for db in range(n // 512):
                    wb = wi % 2
                    wi += 1
                    k.LD(wtm[wb], wv[:, :, db * 512:(db + 1) * 512], r=[], w=[f"wtm{wb}"], eng="pool")
                    for tt in range(NTT):
                        pb = 2 + cnt % 6
                        sb_ = cnt % 4
                        for kc in range(16):
                            k.MM(k.ps[pb][:], hT[:, kc, tt * 128:(tt + 1) * 128], wtm[wb][:, kc, :], kc == 0, kc == 15,
                                 r=[f"wtm{wb}", f"hT{tt}a", f"hT{tt}b"], w=[f"ps{pb}"])
                        k.evac(cnt, stt[sb_], k.ps[pb][:], r=[f"ps{pb}"], w=[f"stt{sb_}"], scale=scale)
                        row0 = mt * MT + tt * 128
                        k.ST(dst[row0:row0 + 128, db * 512:(db + 1) * 512], stt[sb_], r=[f"stt{sb_}"], w=[k.name("dtm")])
                        cnt += 1
            if gate is not None:
                for d in range(2):
                    for tb in range(NTB):
                        pb = 2 + cnt % 6
                        for kc in range(16):
                            k.MM(k.ps[pb][0:16, :], wa1[d][:, kc, :], hT[:, kc, tb * 512:(tb + 1) * 512], kc == 0, kc == 15,
                                 r=[f"wa1{d}"] + hT_res[tb * 8:(tb + 1) * 8], w=[f"ps{pb}"])
                        k.evac(cnt, t1T[d][:, tb * 512:(tb + 1) * 512], k.ps[pb][0:16, :], r=[f"ps{pb}"], w=[f"t1T{d}_{tb}"])
                        cnt += 1
                for tt in range(NTT):
                    for d in range(2):
                        for hf in range(2):
                            pb = 2 + cnt % 6
                            k.MM(k.ps[pb][:], t1T[d][:, tt * 128:(tt + 1) * 128], wa2[d][:, hf * 512:(hf + 1) * 512], True, True,
                                 r=[f"t1T{d}_{tt // 4}", f"wa2{d}"], w=[f"ps{pb}"])
                            S.dve(lambda e, o=pre[d][:, hf * 512:(hf + 1) * 512], i0=k.ps[pb][:], i1=bab[d][:, hf * 512:(hf + 1) * 512]:
                                  e.tensor_tensor(out=o, in0=i0, in1=i1, op=ALU.add),
                                  r=[f"ps{pb}", f"bab{d}"], w=[f"pre{d}_{hf}"])
                            cnt += 1
                        k.ACT(pre[d], pre[d], AF.Exp, r=[f"pre{d}_0", f"pre{d}_1"], w=[f"pre{d}_0", f"pre{d}_1"], scale=-1.0)
                        k.ACT(pre[d], pre[d], AF.Ln, r=[f"pre{d}_0", f"pre{d}_1"], w=[f"pre{d}_0", f"pre{d}_1"],
                              bias=k.one[:, 0:1], scale=1.0)
                        row0 = mt * MT + tt * 128
                        k.ST(gate["dst"][d][row0:row0 + 128, :], pre[d], r=[f"pre{d}_0", f"pre{d}_1"], w=[k.name("dL")])

    def phaseC(self, x_src, x_dst, S_len, m_ap, z_ap, wout_ap, gpost_row, gnorm_row=None):
        k = self
        S = self.S
        k.phase()
        wo = k.sb([128, 32, D], BF16)
        xt = k.sb([128, D], F32)
        mt_ = k.sb([128, E], BF16)
        zt = k.sb([128, E], BF16)
        gT = k.sb([128, 32, 128], BF16)
        tmp = k.sb([128, D], F32)
        gp = k.sb([128, D], F32)
        junk = k.sb([128, D], BF16)
        st = k.sb([128, 16], F32)
        k.LD(gp, gpost_row.partition_broadcast(128), r=[], w=["gp"])
        if gnorm_row is not None:
            gn = k.sb([128, 1024], F32)
            k.LD(gn, gnorm_row.partition_broadcast(128), r=[], w=["gn"])
        wv = wout_ap.rearrange("(kc p) n -> p kc n", p=128)
        for c in range(8):
            k.LD(wo[:, c * 4:(c + 1) * 4, :], wv[:, c * 4:(c + 1) * 4, :], r=[], w=[f"wo{c}"], eng="pool")
        wo_res = [f"wo{c}" for c in range(8)]
        for t in range(S_len // 128):
            r0 = t * 128
            k.LD(mt_, m_ap[r0:r0 + 128, :], r=[], w=["mt"])
            k.LD(zt, z_ap[r0:r0 + 128, :], r=[], w=["zt"])
            k.LD(xt, x_src[r0:r0 + 128, :], r=[], w=["xt"])
            k.ACT(zt, zt, AF.Silu, r=["zt"], w=["zt"])
            if gnorm_row is not None:
                for h in range(4):
                    k.ACT(junk[:, 0:1024], mt_[:, h * 1024:(h + 1) * 1024], AF.Square, r=["mt"], w=["junk", f"gs{h}"],
                          accum_out=st[:, h:h + 1])
                    k.ACT(st[:, h:h + 1], st[:, h:h + 1], AF.Sqrt, r=[f"gs{h}"], w=[f"gs{h}"], bias=k.eps[:, 0:1], scale=1.0 / 1024)
                    S.dve(lambda e, o=st[:, h:h + 1]: e.reciprocal(out=o, in_=o), r=[f"gs{h}"], w=[f"gs{h}"])
                    S.dve(lambda e, o=mt_[:, h * 1024:(h + 1) * 1024], sc=st[:, h:h + 1]: e.scalar_tensor_tensor(
                        out=o, in0=o, scalar=sc, in1=gn, op0=ALU.mult, op1=ALU.mult), r=["mt", f"gs{h}", "gn"], w=["mt"])
            S.dve(lambda e: e.tensor_tensor(out=mt_, in0=mt_, in1=zt, op=ALU.mult), r=["mt", "zt"], w=["mt"])
            for kc in range(32):
                k.TR(k.psb(kc // 8)[:, (kc % 8) * 128:(kc % 8 + 1) * 128], mt_[:, kc * 128:(kc + 1) * 128],
                     r=["mt"], w=[f"ptr{kc // 8}"])
            for q in range(4):
                src = k.psb(q).rearrange("p (a b) -> p a b", a=8)
                dst = gT[:, q * 8:(q + 1) * 8, :]
                if q % 2 == 0:
                    S.act(lambda e, o=dst, i=src: e.copy(out=o, in_=i), r=[f"ptr{q}"], w=[f"gT{q}"])
                else:
                    S.dve(lambda e, o=dst, i=src: e.tensor_copy(out=o, in_=i), r=[f"ptr{q}"], w=[f"gT{q}"])
            for kc in range(32):
                for nb in range(4):
                    k.MM(k.ps[4 + nb][:], gT[:, kc, :], wo[:, kc, nb * 512:(nb + 1) * 512], kc == 0, kc == 31,
                         r=[f"gT{kc // 8}", wo_res[kc // 4]], w=[f"py{nb}"])
            for nb in range(4):
                k.ACT(junk[:, nb * 512:(nb + 1) * 512], k.ps[4 + nb][:], AF.Square, r=[f"py{nb}"], w=[f"junk{nb}", f"ys{nb}"],
                      accum_out=st[:, 4 + nb:5 + nb])
            S.dve(lambda e: e.tensor_reduce(out=st[:, 8:9], in_=st[:, 4:8], axis=AX.X, op=ALU.add),
                  r=[f"ys{nb}" for nb in range(4)], w=["ysum"])
            k.ACT(st[:, 8:9], st[:, 8:9], AF.Sqrt, r=["ysum"], w=["ysum"], bias=k.eps[:, 0:1], scale=1.0 / D)
            S.dve(lambda e: e.reciprocal(out=st[:, 8:9], in_=st[:, 8:9]), r=["ysum"], w=["ysum"])
            for nb in range(4):
                S.dve(lambda e, o=tmp[:, nb * 512:(nb + 1) * 512], i0=k.ps[4 + nb][:], i1=gp[:, nb * 512:(nb + 1) * 512]:
                      e.scalar_tensor_tensor(out=o, in0=i0, scalar=st[:, 8:9], in1=i1, op0=ALU.mult, op1=ALU.mult),
                      r=[f"py{nb}", "ysum", "gp"], w=[f"tmp{nb}"])
            S.dve(lambda e: e.tensor_tensor(out=tmp, in0=tmp, in1=xt, op=ALU.add),
                  r=[f"tmp{nb}" for nb in range(4)] + ["xt"], w=[f"tmp{nb}" for nb in range(4)])
            k.ST(x_dst[r0:r0 + 128, :], tmp, r=[f"tmp{nb}" for nb in range(4)], w=[k.name("xo")])

    def fnet(self, S_len, uT, Zs, Bs, m_out, c_fc, c_m1, c_m2):
        k = self
        S = self.S
        R = S_len // 128
        k.phase()
        fc = k.sb([128, 4, 1024], BF16)
        k.LD(fc, c_fc, r=[], w=["fc"])
        TM_ = min(512, S_len)
        ut = [k.sb([128, 32, TM_], BF16) for _ in range(2)]
        zst = [k.sb([128, 1024], BF16) for _ in range(4)]
        uv = uT.rearrange("(c p) s -> p c s", p=128)
        cnt = 0
        for mtile in range(S_len // TM_):
            b = mtile % 2
            k.LD(ut[b], uv[:, :, mtile * TM_:(mtile + 1) * TM_], r=[], w=[f"ut{b}"])
            for tt in range(TM_ // 128):
                row0 = mtile * TM_ + tt * 128
                for g in range(8):
                    pb = (cnt % 4) * 2
                    zb = cnt % 4
                    for hf in range(2):
                        for kc in range(4):
                            k.MM(k.ps[pb + hf][:], ut[b][:, g * 4 + kc, tt * 128:(tt + 1) * 128], fc[:, kc, hf * 512:(hf + 1) * 512],
                                 kc == 0, kc == 3, r=[f"ut{b}", "fc"], w=[f"ps{pb + hf}"])
                        k.evac(cnt + hf, zst[zb][:, hf * 512:(hf + 1) * 512], k.ps[pb + hf][:], r=[f"ps{pb + hf}"], w=[f"zst{zb}_{hf}"])
                    k.ST(Zs[row0:row0 + 128, g * 1024:(g + 1) * 1024], zst[zb], r=[f"zst{zb}_0", f"zst{zb}_1"], w=[k.name("Z")])
                    cnt += 1
        k.phase()
        zt = [k.sb([R, 8192], BF16) for _ in range(2)]
        m1 = [k.sb([R, 3, R], BF16) for _ in range(2)]
        bst = [k.sb([R, 8192], BF16) for _ in range(2)]
        Zv = Zs.rearrange("(n1 n2) c -> n2 n1 c", n2=128)
        Bv = Bs.rearrange("(k1 n2) c -> n2 k1 c", n2=128)
        cnt = 0
        for n2 in range(128):
            b = n2 % 2
            k.LD(zt[b], Zv[n2], r=[], w=[f"zt{b}"])
            k.LD(m1[b], c_m1[:, n2, :, :], r=[], w=[f"m1{b}"])
            for g in range(8):
                pb = (cnt % 4) * 2
                zr = zt[b][:, g * 1024:g * 1024 + 512]
                zi = zt[b][:, g * 1024 + 512:(g + 1) * 1024]
                C_, S_, nS_ = m1[b][:, 0, :], m1[b][:, 1, :], m1[b][:, 2, :]
                k.MM(k.ps[pb][0:R, :], C_, zr, True, False, r=[f"zt{b}", f"m1{b}"], w=[f"ps{pb}"])
                k.MM(k.ps[pb][0:R, :], S_, zi, False, True, r=[f"zt{b}", f"m1{b}"], w=[f"ps{pb}"])
                k.MM(k.ps[pb + 1][0:R, :], C_, zi, True, False, r=[f"zt{b}", f"m1{b}"], w=[f"ps{pb + 1}"])
                k.MM(k.ps[pb + 1][0:R, :], nS_, zr, False, True, r=[f"zt{b}", f"m1{b}"], w=[f"ps{pb + 1}"])
                k.evac(cnt, bst[b][:, g * 1024:g * 1024 + 512], k.ps[pb][0:R, :], r=[f"ps{pb}"], w=[f"bst{b}_{g}a"])
                k.evac(cnt + 1, bst[b][:, g * 1024 + 512:(g + 1) * 1024], k.ps[pb + 1][0:R, :], r=[f"ps{pb + 1}"], w=[f"bst{b}_{g}b"])
                cnt += 1
            k.ST(Bv[n2], bst[b], r=[f"bst{b}_{g}{h}" for g in range(8) for h in "ab"], w=[k.name("B")])
        k.phase()
        m2 = k.sb([128, 2, 128], BF16)
        k.LD(m2, c_m2, r=[], w=["m2"])
        bt = [k.sb([128, 8192], BF16) for _ in range(2)]
        ost = [k.sb([128, E], BF16) for _ in range(2)]
        Mv = m_out.rearrange("(k2 r) c -> r k2 c", r=R)
        cnt = 0
        for k1 in range(R):
            b = k1 % 2
            k.LD(bt[b], Bs[k1 * 128:(k1 + 1) * 128, :], r=[], w=[f"bt{b}"])
            for g in range(8):
                pb = cnt % 8
                k.MM(k.ps[pb][:], m2[:, 0, :], bt[b][:, g * 1024:g * 1024 + 512], True, False, r=[f"bt{b}", "m2"], w=[f"ps{pb}"])
                k.MM(k.ps[pb][:], m2[:, 1, :], bt[b][:, g * 1024 + 512:(g + 1) * 1024], False, True, r=[f"bt{b}", "m2"], w=[f"ps{pb}"])
                k.evac(cnt, ost[b][:, g * 512:(g + 1) * 512], k.ps[pb][:], r=[f"ps{pb}"], w=[f"ost{b}_{g}"])
                cnt += 1
            k.ST(Mv[k1], ost[b], r=[f"ost{b}_{g}" for g in range(8)], w=[k.name("M")])

    def nat(self, S_len, qT, kT, v, m_out, c_bias):
        k = self
        S = self.S
        rows = S_len // GRID_W
        k.phase()
        bias_int = k.sb([128, 32, 576], BF16)
        bias_edge = k.sb([128, 32, 576], BF16)
        k.LD(bias_int, c_bias[2], r=[], w=["bias_int"])
        qt = k.sb([128, 32, 128], BF16)
        kt = k.sb([128, 32, 576], BF16)
        vt = k.sb([128, 5, E], BF16)
        ssb = [k.sb([128, 576], F32) for _ in range(2)]
        pb_ = [k.sb([128, 640], BF16) for _ in range(2)]
        pT = [k.sb([128, 5, 128], BF16) for _ in range(2)]
        ost = k.sb([128, E], BF16)
        stat = k.sb([128, 8], F32)
        qv = qT.rearrange("(h p) s -> p h s", p=128)
        kv = kT.rearrange("(h p) s -> p h s", p=128)
        scale = 128.0 ** -0.5
        cnt = 0
        for i in range(rows // 2):
            kb = min(max(2 * i - 4, 0), rows - 9)
            d = 2 * i - kb
            if d == 4:
                bias = bias_int
                bres = "bias_int"
            else:
                k.LD(bias_edge, c_bias[NAT_PIDX[d]], r=[], w=["bias_edge"])
                bias = bias_edge
                bres = "bias_edge"
            k.LD(qt, qv[:, :, i * 128:(i + 1) * 128], r=[], w=["qt"])
            k.LD(kt, kv[:, :, kb * 64:kb * 64 + 576], r=[], w=["kt"])
            for c in range(4):
                k.LD(vt[:, c, :], v[kb * 64 + c * 128:kb * 64 + (c + 1) * 128, :], r=[], w=[f"vt{c}"])
            k.LD(vt[0:64, 4, :], v[kb * 64 + 512:kb * 64 + 576, :], r=[], w=["vt4"])
            for h in range(32):
                b = cnt % 2
                p0 = 0 if b == 0 else 2
                ptb = 4 + b
                pob = 6 + b
                k.MM(k.ps[p0][:], qt[:, h, :], kt[:, h, 0:512], True, True, r=["qt", "kt"], w=[f"ps{p0}"])
                k.MM(k.ps[p0 + 1][:, 0:64], qt[:, h, :], kt[:, h, 512:576], True, True, r=["qt", "kt"], w=[f"ps{p0 + 1}"])
                if NATDBG < 2:
                    cnt += 1
                    continue
                S.dve(lambda e, o=ssb[b][:, 0:512], i0=k.ps[p0][:], i1=bias[:, h, 0:512]:
                      e.scalar_tensor_tensor(out=o, in0=i0, scalar=scale, in1=i1, op0=ALU.mult, op1=ALU.add),
                      r=[f"ps{p0}", bres], w=[f"ssb{b}a"])
                S.dve(lambda e, o=ssb[b][:, 512:576], i0=k.ps[p0 + 1][:, 0:64], i1=bias[:, h, 512:576]:
                      e.scalar_tensor_tensor(out=o, in0=i0, scalar=scale, in1=i1, op0=ALU.mult, op1=ALU.add),
                      r=[f"ps{p0 + 1}", bres], w=[f"ssb{b}b"])
                if NATDBG < 3:
                    cnt += 1
                    continue
                S.dve(lambda e, o=stat[:, b:b + 1], i=ssb[b]: e.tensor_reduce(out=o, in_=i, axis=AX.X, op=ALU.max),
                      r=[f"ssb{b}a", f"ssb{b}b"], w=[f"mx{b}"])
                S.dve(lambda e, o=stat[:, b:b + 1]: e.tensor_scalar_mul(out=o, in0=o, scalar1=-1.0), r=[f"mx{b}"], w=[f"mx{b}"])
                if NATDBG < 4:
                    cnt += 1
                    continue
                k.ACT(pb_[b][:, 0:576], ssb[b], AF.Exp, r=[f"ssb{b}a", f"ssb{b}b", f"mx{b}"], w=[f"pb{b}", f"rs{b}"],
                      bias=stat[:, b:b + 1], scale=1.0, accum_out=stat[:, 2 + b:3 + b])
                S.dve(lambda e, o=stat[:, 2 + b:3 + b]: e.reciprocal(out=o, in_=o), r=[f"rs{b}"], w=[f"rs{b}"])
                if NATDBG < 5:
                    cnt += 1
                    continue
                for c in range(5):
                    if c < 4:
                        k.TR(k.psb(ptb)[:, c * 128:(c + 1) * 128], pb_[b][:, c * 128:(c + 1) * 128], r=[f"pb{b}"], w=[f"ps{ptb}"])
                    else:
                        k.TR(k.psb(ptb)[0:64, c * 128:(c + 1) * 128], pb_[b][:, 512:576], r=[f"pb{b}"], w=[f"ps{ptb}"])
                if NATDBG < 6:
                    cnt += 1
                    continue
                src4 = k.psb(ptb)[:, 0:512].rearrange("p (a b) -> p a b", a=4)
                if cnt % 2 == 0:
                    S.act(lambda e, o=pT[b][:, 0:4, :], i=src4: e.copy(out=o, in_=i), r=[f"ps{ptb}"], w=[f"pT{b}a"])
                    S.act(lambda e, o=pT[b][0:64, 4, :], i=k.psb(ptb)[0:64, 512:640]: e.copy(out=o, in_=i), r=[f"ps{ptb}"], w=[f"pT{b}b"])
                else:
                    S.dve(lambda e, o=pT[b][:, 0:4, :], i=src4: e.tensor_copy(out=o, in_=i), r=[f"ps{ptb}"], w=[f"pT{b}a"])
                    S.dve(lambda e, o=pT[b][0:64, 4, :], i=k.psb(ptb)[0:64, 512:640]: e.tensor_copy(out=o, in_=i), r=[f"ps{ptb}"], w=[f"pT{b}b"])
                if NATDBG < 7:
                    cnt += 1
                    continue
                for c in range(5):
                    if c < 4:
                        k.MM(k.ps[pob][:, 0:128], pT[b][:, c, :], vt[:, c, h * 128:(h + 1) * 128], c == 0, False,
                             r=[f"pT{b}a", f"vt{c}"], w=[f"ps{pob}"])
                    else:
                        k.MM(k.ps[pob][:, 0:128], pT[b][0:64, 4, :], vt[0:64, 4, h * 128:(h + 1) * 128], False, True,
                             r=[f"pT{b}b", "vt4"], w=[f"ps{pob}"])
                if NATDBG < 8:
                    cnt += 1
                    continue
                if cnt % 2 == 0:
                    k.ACT(ost[:, h * 128:(h + 1) * 128], k.ps[pob][:, 0:128], AF.Copy, r=[f"ps{pob}", f"rs{b}"], w=[f"ost{h}"],
                          scale=stat[:, 2 + b:3 + b])
                else:
                    S.dve(lambda e, o=ost[:, h * 128:(h + 1) * 128], i=k.ps[pob][:, 0:128], sc=stat[:, 2 + b:3 + b]:
                          e.tensor_scalar_mul(out=o, in0=i, scalar1=sc), r=[f"ps{pob}", f"rs{b}"], w=[f"ost{h}"])
                cnt += 1
            k.ST(m_out[i * 128:(i + 1) * 128, :], ost, r=[f"ost{h}" for h in range(32)], w=[k.name("mo")])

    def gla(self, S_len, q, kk, v, Lf, Lb, of_s, m_out, c_mats, c_masks):
        k = self
        S = self.S
        n = S_len // 128
        k.phase()
        mats = k.sb([128, 6, 128], F32)
        masks = k.sb([128, 2, 128], F32)
        onec = k.sb([128, 1], F32)
        k.LD(mats, c_mats, r=[], w=["mats"])
        k.LD(masks, c_masks, r=[], w=["masks"])
        S.dve(lambda e: e.memset(onec, -1.0 / 16.0), r=[], w=["onec"])
        qt = k.sb([128, 1024], BF16)
        ktl = k.sb([128, 1024], BF16)
        vt = k.sb([128, E], BF16)
        Lt = k.sb([128, 1024], F32)
        Ex = [k.sb([128, 1024], F32) for _ in range(4)]
        prod = [k.sb([128, 1024], BF16) for _ in range(4)]
        Tt = k.sb([128, 24, 128], BF16)
        sT = [k.sb([128, 128], BF16) for _ in range(2)]
        dec = k.sb([128, 8], F32)
        st32 = k.sb([128, 8, 1024], F32)
        st16 = k.sb([128, 8, 1024], BF16)
        ost = k.sb([128, E], BF16)
        oft = k.sb([128, E], BF16)
        for di in range(2):
            Ld = Lf if di == 0 else Lb
            S.dve(lambda e: e.memset(st32, 0.0), r=[f"st32_{j}" for j in range(8)], w=[f"st32_{j}" for j in range(8)])
            S.dve(lambda e: e.memset(st16, 0.0), r=[f"st16_{j}" for j in range(8)], w=[f"st16_{j}" for j in range(8)])
            order = range(n) if di == 0 else range(n - 1, -1, -1)
            for c in order:
                r0 = c * 128
                k.LD(qt, q[r0:r0 + 128, :], r=[], w=["qt"])
                k.LD(ktl, kk[r0:r0 + 128, :], r=[], w=["ktl"])
                k.LD(vt, v[r0:r0 + 128, :], r=[], w=["vt"])
                k.LD(Lt, Ld[r0:r0 + 128, :], r=[], w=["Lt"])
                if di == 1:
                    k.LD(oft, of_s[r0:r0 + 128, :], r=[f"ofd{c}"], w=["oft"])
                for gi, (mi, outs) in enumerate([(0, [(0, 1.0), (1, -1.0)]), (1, [(2, 1.0)]), (2, [(3, 1.0)])]):
                    for hf in range(2):
                        k.MM(k.ps[hf][:], mats[:, di * 3 + mi, :], Lt[:, hf * 512:(hf + 1) * 512], True, True,
                             r=["mats", "Lt"], w=[f"ps{hf}"])
                    for (ei, sgn) in outs:
                        for hf in range(2):
                            k.ACT(Ex[ei][:, hf * 512:(hf + 1) * 512], k.ps[hf][:], AF.Exp, r=[f"ps{hf}"], w=[f"Ex{ei}_{hf}"], scale=sgn)
                for j in range(8):
                    k.MM(k.ps[2][:, j:j + 1], Lt[:, j * 128:(j + 1) * 128], onec, True, True, r=["Lt", "onec"], w=["ps2"])
                k.ACT(dec, k.ps[2][:, 0:8], AF.Exp, r=["ps2"], w=["dec"])
                for pi, (src, ei, sres) in enumerate([(qt, 0, "qt"), (ktl, 1, "ktl"), (qt, 2, "qt"), (ktl, 3, "ktl")]):
                    S.dve(lambda e, o=prod[pi], a=src, b_=Ex[ei]: e.tensor_tensor(out=o, in0=a, in1=b_, op=ALU.mult),
                          r=[sres, f"Ex{ei}_0", f"Ex{ei}_1"], w=[f"prod{pi}"])
                for ti, pi in enumerate([0, 1, 2]):
                    for j in range(8):
                        k.TR(k.psb(3)[:, j * 128:(j + 1) * 128], prod[pi][:, j * 128:(j + 1) * 128], r=[f"prod{pi}"], w=["ps3"])
                    srcv = k.psb(3).rearrange("p (a b) -> p a b", a=8)
                    if ti % 2 == 0:
                        S.act(lambda e, o=Tt[:, ti * 8:(ti + 1) * 8, :], i=srcv: e.copy(out=o, in_=i), r=["ps3"], w=[f"Tt{ti}"])
                    else:
                        S.dve(lambda e, o=Tt[:, ti * 8:(ti + 1) * 8, :], i=srcv: e.tensor_copy(out=o, in_=i), r=["ps3"], w=[f"Tt{ti}"])
                for h in range(4):
                    sb_ = h % 2
                    for kt_ in range(2):
                        k.MM(k.ps[4][:, 0:128], Tt[:, 8 + h * 2 + kt_, :], Tt[:, h * 2 + kt_, :], kt_ == 0, kt_ == 1,
                             r=["Tt0", "Tt1"], w=["ps4"])
                    S.dve(lambda e, o=sT[sb_], i0=k.ps[4][:, 0:128], i1=masks[:, di, :]: e.tensor_tensor(out=o, in0=i0, in1=i1, op=ALU.mult),
                          r=["ps4", "masks"], w=[f"sT{sb_}"])
                    for blk in range(2):
                        pob = 5 + blk
                        vsl = vt[:, h * 1024 + blk * 512:h * 1024 + (blk + 1) * 512]
                        k.MM(k.ps[pob][:], sT[sb_], vsl, True, False, r=[f"sT{sb_}", "vt"], w=[f"ps{pob}"])
                        for kt_ in range(2):
                            k.MM(k.ps[pob][:], Tt[:, 16 + h * 2 + kt_, :], st16[:, h * 2 + kt_, blk * 512:(blk + 1) * 512], False, kt_ == 1,
                                 r=["Tt2", f"st16_{h * 2 + kt_}"], w=[f"ps{pob}"])
                        osl = ost[:, h * 1024 + blk * 512:h * 1024 + (blk + 1) * 512]
                        if di == 0:
                            k.evac(blk, osl, k.ps[pob][:], r=[f"ps{pob}"], w=[f"ost{h}_{blk}"])
                        else:
                            S.dve(lambda e, o=osl, i0=k.ps[pob][:], i1=oft[:, h * 1024 + blk * 512:h * 1024 + (blk + 1) * 512]:
                                  e.tensor_tensor(out=o, in0=i0, in1=i1, op=ALU.add), r=[f"ps{pob}", "oft"], w=[f"ost{h}_{blk}"])
                    for kt_ in range(2):
                        j = h * 2 + kt_
                        for blk in range(2):
                            k.MM(k.ps[7][:], prod[3][:, j * 128:(j + 1) * 128], vt[:, h * 1024 + blk * 512:h * 1024 + (blk + 1) * 512],
                                 True, True, r=["prod3", "vt"], w=["ps7"])
                            S.dve(lambda e, o=st32[:, j, blk * 512:(blk + 1) * 512], sc=dec[:, j:j + 1], i1=k.ps[7][:]:
                                  e.scalar_tensor_tensor(out=o, in0=o, scalar=sc, in1=i1, op0=ALU.mult, op1=ALU.add),
                                  r=["ps7", "dec", f"st32_{j}"], w=[f"st32_{j}"])
                        S.act(lambda e, o=st16[:, j, :], i=st32[:, j, :]: e.copy(out=o, in_=i), r=[f"st32_{j}"], w=[f"st16_{j}"])
                dst = of_s if di == 0 else m_out
                k.ST(dst[r0:r0 + 128, :], ost, r=[f"ost{h}_{blk}" for h in range(4) for blk in range(2)],
                     w=[f"ofd{c}" if di == 0 else k.name("go")])


def build_program(unit_lens, layers, out_prompt_rows=None, debug=None):
    Smax = max(unit_lens)
    kb = K(None, layers)
    nc = kb.nc
    S = kb.S
    dt_in = {}

    def ext(name, shape, dt):
        dt_in[name] = (shape, dt)
        return nc.dram_tensor(name, shape, dt, kind="ExternalInput").ap()

    x_in = [ext(f"x{u}", [L, D], F32) for u, L in enumerate(unit_lens)]
    npre = ext("norm_pre_g", [4, D], F32)
    npost = ext("norm_post_g", [4, D], F32)
    fnet_w_in = ext("fnet_w_in", [2, D, 2 * E], F32)
    fnet_w_out = ext("fnet_w_out", [2, E, D], F32)
    nat_w_in = ext("nat_w_in", [1, D, 4 * E], F32)
    nat_w_out = ext("nat_w_out", [1, E, D], F32)
    gla_w_in = ext("gla_w_in", [1, D, 2048 + 2 * E], F32)
    gla_w_out = ext("gla_w_out", [1, E, D], F32)
    wa1 = [ext("gla_wa1_f", [1, D, 16], F32), ext("gla_wa1_b", [1, D, 16], F32)]
    wa2 = [ext("gla_wa2_f", [1, 16, 1024], F32), ext("gla_wa2_b", [1, 16, 1024], F32)]
    ba = [ext("gla_ba_f", [1, 1024], F32), ext("gla_ba_b", [1, 1024], F32)]
    gnorm = ext("gla_g_norm", [1, 1024], F32)
    c_ident = ext("c_ident", [128, 128], BF16)
    c_fc = ext("c_fc", [128, 4, 1024], BF16)
    c_m2 = ext("c_m2", [128, 2, 128], BF16)
    c_m1 = {}
    for L in sorted(set(unit_lens)):
        R = L // 128
        c_m1[L] = ext(f"c_m1_{L}", [R, 128, 3, R], BF16)
    c_bias = ext("c_bias", [5, 128, 32, 576], BF16)
    c_mats = ext("c_gmats", [128, 6, 128], F32)
    c_masks = ext("c_gmasks", [128, 2, 128], F32)

    outs = []
    x_st = []
    for u, L in enumerate(unit_lens):
        if u == 0 and out_prompt_rows is not None:
            st_ap = nc.dram_tensor("xp_state", [L, D], F32).ap()
            o = nc.dram_tensor("y0", [out_prompt_rows, D], F32, kind="ExternalOutput").ap()
            outs.append(o)
        else:
            st_ap = nc.dram_tensor(f"y{u}", [L, D], F32, kind="ExternalOutput").ap()
            outs.append(st_ap)
        x_st.append(st_ap)
    scrA = nc.dram_tensor("scrA", [E, Smax], BF16).ap()
    scrB = nc.dram_tensor("scrB", [E, Smax], BF16).ap()
    scrV = nc.dram_tensor("scrV", [Smax, E], BF16).ap()
    scrZ = nc.dram_tensor("scrZ", [Smax, E], BF16).ap()
    scrM = nc.dram_tensor("scrM", [Smax, E], BF16).ap()
    scrO = nc.dram_tensor("scrO", [Smax, E], BF16).ap()
    scrQ = nc.dram_tensor("scrQ", [Smax, 1024], BF16).ap()
    scrK = nc.dram_tensor("scrK", [Smax, 1024], BF16).ap()
    scrL = [nc.dram_tensor("scrLf", [Smax, 1024], F32).ap(), nc.dram_tensor("scrLb", [Smax, 1024], F32).ap()]
    scrZZ = nc.dram_tensor("scrZZ", [Smax, 8192], BF16).ap()
    scrBB = nc.dram_tensor("scrBB", [Smax, 8192], BF16).ap()

    with contextlib.ExitStack() as st:
        kb.setup_mem(st)
        kb.LD(kb.ident, c_ident, r=[], w=["ident"])
        S.dve(lambda e: e.memset(kb.eps, EPS), r=[], w=["eps"])
        S.dve(lambda e: e.memset(kb.one, 1.0), r=[], w=["one"])
        for li, (kind, j) in enumerate(layers if True else []):
          try:
            for u, L in enumerate(unit_lens):
                xs = x_in[u] if li == 0 else x_st[u]
                xd = x_st[u]
                if kind == "fnet":
                    w = fnet_w_in[j]
                    kb.phaseA(xs, L, npre[li:li + 1, :], fm=[(w[:, 0:E], scrA[:, 0:L])], tm=[(w[:, E:2 * E], scrZ[0:L, :], 1.0)])
                    kb.fnet(L, scrA[:, 0:L], scrZZ[0:L, :], scrBB[0:L, :], scrM[0:L, :], c_fc, c_m1[L], c_m2)
                    kb.phaseC(xs, xd, L, scrM[0:L, :], scrZ[0:L, :], fnet_w_out[j], npost[li:li + 1, :])
                elif kind == "nat":
                    w = nat_w_in[j]
                    kb.phaseA(xs, L, npre[li:li + 1, :], fm=[(w[:, 0:E], scrA[:, 0:L]), (w[:, E:2 * E], scrB[:, 0:L])],
                              tm=[(w[:, 2 * E:3 * E], scrV[0:L, :], 1.0), (w[:, 3 * E:4 * E], scrZ[0:L, :], 1.0)])
                    kb.nat(L, scrA[:, 0:L], scrB[:, 0:L], scrV[0:L, :], scrM[0:L, :], c_bias)
                    kb.phaseC(xs, xd, L, scrM[0:L, :], scrZ[0:L, :], nat_w_out[j], npost[li:li + 1, :])
                else:
                    w = gla_w_in[j]
                    gate = dict(wa1=[wa1[0][j], wa1[1][j]], wa2=[wa2[0][j], wa2[1][j]],
                                ba=[ba[0][j:j + 1, :], ba[1][j:j + 1, :]], dst=[scrL[0][0:L, :], scrL[1][0:L, :]])
                    kb.phaseA(xs, L, npre[li:li + 1, :], fm=[],
                              tm=[(w[:, 0:1024], scrQ[0:L, :], 1.0 / 16.0), (w[:, 1024:2048], scrK[0:L, :], 1.0),
                                  (w[:, 2048:2048 + E], scrV[0:L, :], 1.0), (w[:, 2048 + E:2048 + 2 * E], scrZ[0:L, :], 1.0)],
                              gate=gate)
                    kb.gla(L, scrQ[0:L, :], scrK[0:L, :], scrV[0:L, :], scrL[0][0:L, :], scrL[1][0:L, :], scrO[0:L, :], scrM[0:L, :],
                           c_mats, c_masks)
                    kb.phaseC(xs, xd, L, scrM[0:L, :], scrZ[0:L, :], gla_w_out[j], npost[li:li + 1, :], gnorm_row=gnorm[j:j + 1, :])
          except StopBuild:
            break
        if out_prompt_rows is not None:
            kb.phase()
            pidc = {}

            def final_copy(e):
                pid = e.partition_id()
                return e.dma_start(out=outs[0], in_=x_st[0][bass.ds(pid * out_prompt_rows, out_prompt_rows), :])
            S.dma("pool", final_copy, r=[], w=["y0"])
        S.emit(nc, st)
    return nc, dt_in


def host_consts(inputs, unit_lens):
    c = {}
    c["c_ident"] = np.eye(128, dtype=np.float32).astype(NPBF)
    fc = None
    for L in sorted(set(unit_lens)):
        fc, m1, m2 = fnet_consts(L)
        c[f"c_m1_{L}"] = m1
        c["c_m2"] = m2
    c["c_fc"] = fc
    c["c_bias"] = nat_bias_tables(np.asarray(inputs["nat_rpb"])[0])
    mats, masks = gla_consts()
    c["c_gmats"] = mats
    c["c_gmasks"] = masks
    return c


WKEYS = ["norm_pre_g", "norm_post_g", "fnet_w_in", "fnet_w_out", "nat_w_in", "nat_w_out", "gla_w_in", "gla_w_out",
         "gla_wa1_f", "gla_wa1_b", "gla_wa2_f", "gla_wa2_b", "gla_ba_f", "gla_ba_b", "gla_g_norm"]


def kernel(**inputs):
    x_prompt = np.asarray(inputs["x_prompt"], np.float32)
    x_sample = np.asarray(inputs["x_sample"], np.float32)
    unit_lens = [x_prompt.shape[1], x_sample.shape[1], x_sample.shape[1]]
    nc, _ = build_program(unit_lens, LAYERS, out_prompt_rows=x_prompt.shape[1] // 8)
    base = {k_: np.ascontiguousarray(np.asarray(inputs[k_], np.float32)) for k_ in WKEYS}
    base.update(host_consts(inputs, unit_lens))
    in_maps = []
    for c in range(8):
        m = dict(base)
        m["x0"] = np.ascontiguousarray(x_prompt[0])
        m["x1"] = np.ascontiguousarray(x_sample[2 * c])
        m["x2"] = np.ascontiguousarray(x_sample[2 * c + 1])
        in_maps.append(m)
    res = run_bass_kernel_spmd(nc, in_maps, core_ids=list(range(8)))
    yp = np.concatenate([res.results[c]["y0"] for c in range(8)], axis=0)[None]
    ys = np.stack([res.results[c][f"y{u}"] for c in range(8) for u in (1, 2)], axis=0)
    return (yp.astype(np.float32), ys.astype(np.float32))
```

```python
import contextlib
import numpy as np
import ml_dtypes
import concourse.bass as bass
import concourse.mybir as mybir
from concourse.bass_utils import run_bass_kernel_spmd

F32 = mybir.dt.float32
BF16 = mybir.dt.bfloat16
AF = mybir.ActivationFunctionType
ALU = mybir.AluOpType
AX = mybir.AxisListType
NPBF = ml_dtypes.bfloat16

D = 2048
E = 4096
EPS = 1e-6
GRID_W = 64
LAYERS = [("fnet", 0), ("nat", 0), ("gla", 0), ("fnet", 1)]

ENGS = ["pe", "act", "dve", "pool", "sp"]
SIG_CHUNK = 30000
DMA_SLOTS = 8
DMA_CHUNK = 1800


class Op:
    __slots__ = ("eng", "fn", "deps", "dma", "pos", "sig", "semidx", "slot", "dcount", "waits")

    def __init__(self):
        self.sig = False
        self.dma = False
        self.waits = ()


class Sched:
    def __init__(self):
        self.ops = []
        self.last_w = {}
        self.rd_eng = {}
        self.rd_dma = {}
        self.streams = {e: [] for e in ENGS}
        self.ndma = {e: 0 for e in ENGS}
        self.slot_last = {}
        self.last_compute = {}
        self.pending = {}

    def barrier(self):
        deps = list(self.last_compute.values()) + list(self.slot_last.values())
        self.pending = {e: list(deps) for e in ENGS}
        self.last_w = {}
        self.rd_eng = {}
        self.rd_dma = {}

    def op(self, eng, fn, r=(), w=(), dma=False):
        o = Op()
        o.eng = eng
        o.fn = fn
        o.dma = dma
        deps = []
        pb = self.pending.pop(eng, None)
        if pb:
            deps.extend(pb)
        for x in r:
            lw = self.last_w.get(x)
            if lw is not None:
                deps.append(lw)
        for x in w:
            lw = self.last_w.get(x)
            if lw is not None:
                deps.append(lw)
            de = self.rd_eng.get(x)
            if de:
                deps.extend(de.values())
            dd = self.rd_dma.get(x)
            if dd:
                deps.extend(dd)
        if dma:
            j = self.ndma[eng]
            self.ndma[eng] = j + 1
            slot = j % DMA_SLOTS
            o.slot = slot
            prev = self.slot_last.get((eng, slot))
            o.dcount = (prev.dcount + 1) if prev is not None else 1
            if prev is not None:
                deps.append(prev)
            self.slot_last[(eng, slot)] = o
        else:
            self.last_compute[eng] = o
        o.deps = deps
        for x in w:
            self.last_w[x] = o
            self.rd_eng[x] = {}
            self.rd_dma[x] = []
        for x in r:
            if dma:
                self.rd_dma.setdefault(x, []).append(o)
            else:
                self.rd_eng.setdefault(x, {})[eng] = o
        o.pos = len(self.streams[eng])
        self.streams[eng].append(o)
        self.ops.append(o)
        return o

    def pe(self, fn, r=(), w=()):
        return self.op("pe", fn, r, w)

    def act(self, fn, r=(), w=()):
        return self.op("act", fn, r, w)

    def dve(self, fn, r=(), w=()):
        return self.op("dve", fn, r, w)

    def dma(self, eng, fn, r=(), w=()):
        return self.op(eng, fn, r, w, dma=True)

    def plan(self):
        seen = {e: {} for e in ENGS}
        for o in self.ops:
            waits = []
            se = seen[o.eng]
            for d in o.deps:
                if d.dma:
                    key = ("d", d.eng, d.slot)
                    need = d.dcount
                else:
                    if d.eng == "pe" and o.eng == "pe" and not o.dma:
                        continue
                    key = ("e", d.eng)
                    need = d.pos
                if se.get(key, -1) >= need:
                    continue
                se[key] = need
                if not d.dma:
                    d.sig = True
                waits.append(d)
            o.waits = waits
            o.deps = None
        self.nsig = {}
        for e in ENGS:
            c = 0
            for o in self.streams[e]:
                if o.sig:
                    o.semidx = c
                    c += 1
            self.nsig[e] = c

    def emit(self, nc, stack):
        self.plan()
        esem = {}
        for e in ENGS:
            n = (self.nsig[e] + SIG_CHUNK - 1) // SIG_CHUNK
            esem[e] = [stack.enter_context(nc.semaphore(f"s_{e}_{i}")) for i in range(max(n, 1))]
        dsem = {}
        for (e, slot), last in self.slot_last.items():
            n = (last.dcount + DMA_CHUNK - 1) // DMA_CHUNK
            dsem[(e, slot)] = [stack.enter_context(nc.semaphore(f"d_{e}_{slot}_{i}")) for i in range(n)]

        def sem_of(d):
            if d.dma:
                k = (d.dcount - 1) // DMA_CHUNK
                return dsem[(d.eng, d.slot)][k], 16 * (d.dcount - k * DMA_CHUNK)
            k = d.semidx // SIG_CHUNK
            return esem[d.eng][k], d.semidx - k * SIG_CHUNK + 1

        def run(engobj, ename):
            for o in self.streams[ename]:
                for d in o.waits[:-1]:
                    s, v = sem_of(d)
                    engobj.wait_ge(s, v)
                ins = o.fn(engobj)
                if o.waits:
                    s, v = sem_of(o.waits[-1])
                    ins._wait_ge(s, v)
                if o.dma:
                    s, _ = sem_of(o)
                    ins.then_inc(s, 16)
                elif o.sig:
                    s, _ = sem_of(o)
                    ins.then_inc(s, 1)
                o.fn = None
            for (e, slot), last in self.slot_last.items():
                if e == ename:
                    s, v = sem_of(last)
                    engobj.wait_ge(s, v)

        block = stack.enter_context(nc.Block())

        @block.tensor
        def _(e):
            run(e, "pe")

        @block.scalar
        def _(e):
            run(e, "act")

        @block.vector
        def _(e):
            run(e, "dve")

        @block.gpsimd
        def _(e):
            run(e, "pool")

        @block.sync
        def _(e):
            run(e, "sp")


def fnet_consts(S_len):
    R = S_len // 128
    c = np.arange(512)
    ang = 2 * np.pi * np.outer(c, c) / 512.0
    fc = np.concatenate([np.cos(ang), -np.sin(ang)], axis=1) / np.sqrt(512.0)
    fc = fc.reshape(4, 128, 1024).transpose(1, 0, 2)
    n2 = np.arange(128)[:, None, None]
    n1 = np.arange(R)[None, :, None]
    k1 = np.arange(R)[None, None, :]
    th = 2 * np.pi * ((k1 * (128 * n1 + n2)) % S_len) / float(S_len)
    m1 = np.stack([np.cos(th), np.sin(th), -np.sin(th)], axis=2) / np.sqrt(float(R))
    m1 = m1.transpose(1, 0, 2, 3)
    a = np.arange(128)
    ph = 2 * np.pi * np.outer(a, a) / 128.0
    m2 = np.stack([np.cos(ph), np.sin(ph)], axis=1) / np.sqrt(128.0)
    return fc.astype(NPBF), np.ascontiguousarray(m1).astype(NPBF), m2.astype(NPBF)


NAT_PAT = {0: (0, 0), 2: (0, 0), 4: (0, 1), 5: (1, 1), 7: (1, 1)}
NAT_PIDX = {0: 0, 2: 1, 4: 2, 5: 3, 7: 4}


def nat_bias_tables(rpb):
    out = np.full((5, 128, 32, 576), -30000.0, np.float32)
    c = np.arange(64)
    ws = np.clip(c - 8, 0, 48)
    kc = np.arange(64)
    valid_c = (kc[None, :] >= ws[:, None]) & (kc[None, :] < ws[:, None] + 16)
    colidx = np.clip(kc[None, :] - c[:, None], -15, 15) + 15
    for d, (r0, r1) in NAT_PAT.items():
        p = NAT_PIDX[d]
        for rr, rs_rel in enumerate((r0, r1)):
            for kr in range(9):
                if not (rs_rel <= kr < rs_rel + 8):
                    continue
                ridx = kr - d - rr + 7
                vals = rpb[:, ridx, :][:, colidx]
                blk = out[p, rr * 64:(rr + 1) * 64, :, kr * 64:(kr + 1) * 64]
                blk[:] = np.where(valid_c[:, None, :], vals.transpose(1, 0, 2), -30000.0)
    return out.astype(NPBF)


def gla_consts():
    j = np.arange(128)[:, None]
    i = np.arange(128)[None, :]
    sc = -1.0 / 16.0
    f_cum = (j <= i).astype(np.float32)
    f_mid = (j <= 63).astype(np.float32) * np.ones((1, 128), np.float32)
    f_rest = (j > i).astype(np.float32)
    b_cum = (j >= i).astype(np.float32)
    b_mid = (j >= 64).astype(np.float32) * np.ones((1, 128), np.float32)
    b_rest = (j < i).astype(np.float32)
    mats = np.stack([(f_cum - f_mid) * sc, f_cum * sc, f_rest * sc,
                     (b_cum - b_mid) * sc, b_cum * sc, b_rest * sc], axis=1).astype(np.float32)
    masks = np.stack([(j <= i), (j > i)], axis=1).astype(np.float32)
    return mats, masks


MAX_PHASE = 0
NATDBG = 9


class StopBuild(Exception):
    pass


class K:
    def __init__(self, units, layers, arena_elems=100352):
        self.nc = bass.Bass("TRN2", target_bir_lowering=False)
        self.S = Sched()
        self.units = units
        self.layers = layers
        self.arena_elems = arena_elems
        self.uid = 0

    def setup_mem(self, st):
        nc = self.nc
        self.arena = st.enter_context(nc.sbuf_tensor("arena", [128, self.arena_elems], BF16))
        self.ps = [st.enter_context(nc.psum_tensor(f"ps{i}", [128, 512], F32)) for i in range(8)]
        self.off = 0
        self.ident = self.sb([128, 128], BF16)
        self.eps = self.sb([128, 1], F32)
        self.one = self.sb([128, 1], F32)
        self.persist_end = self.off

    def sb(self, shape, dt):
        n = int(np.prod(shape[1:]))
        nb = n * (4 if dt == F32 else 2)
        off = (self.off + 63) // 64 * 64
        assert off + nb <= self.arena_elems * 2, f"arena overflow {off + nb}"
        v = self.arena[0:shape[0], off // 2:(off + nb) // 2]
        if dt == F32:
            v = v.bitcast(F32)
        if len(shape) == 3:
            v = v.rearrange("p (a b) -> p a b", a=shape[1])
        self.off = off + nb
        return v

    def psb(self, i):
        return self.ps[i][:].bitcast(BF16)

    def phase(self):
        self.nphase = getattr(self, "nphase", 0) + 1
        if MAX_PHASE and self.nphase > MAX_PHASE:
            raise StopBuild()
        self.S.barrier()
        self.off = self.persist_end

    def name(self, s):
        self.uid += 1
        return f"{s}#{self.uid}"

    def MM(self, out, lhsT, rhs, start, stop, r, w):
        self.S.pe(lambda e: e.matmul(out, lhsT=lhsT, rhs=rhs, start=start, stop=stop), r=r, w=w)

    def TR(self, out, in_, r, w, ident=None):
        idn = self.ident if ident is None else ident
        self.S.pe(lambda e: e.transpose(out=out, in_=in_, identity=idn), r=r, w=w)

    def ACT(self, out, in_, func, r, w, **kw):
        self.S.act(lambda e: e.activation(out=out, in_=in_, func=func, **kw), r=r, w=w)

    def LD(self, out, in_, r, w, eng="sp"):
        self.S.dma(eng, lambda e: e.dma_start(out=out, in_=in_), r=r, w=w)

    def ST(self, out, in_, r, w, eng="pool"):
        self.S.dma(eng, lambda e: e.dma_start(out=out, in_=in_), r=r, w=w)

    def evac(self, idx, out, in_, r, w, scale=1.0):
        if idx % 2 == 0:
            self.S.act(lambda e: e.activation(out=out, in_=in_, func=AF.Copy, scale=float(scale)), r=r, w=w)
        else:
            if scale == 1.0:
                self.S.dve(lambda e: e.tensor_copy(out=out, in_=in_), r=r, w=w)
            else:
                self.S.dve(lambda e: e.tensor_scalar_mul(out=out, in0=in_, scalar1=float(scale)), r=r, w=w)

    def phaseA(self, x_src, S_len, g_row, fm, tm, gate=None):
        k = self
        S = self.S
        k.phase()
        MT = min(2048, S_len)
        NTT = MT // 128
        NTB = MT // 512
        g_sb = k.sb([128, D], F32)
        xt = [k.sb([128, D], F32) for _ in range(2)]
        junk = k.sb([128, D], BF16)
        hb = [k.sb([128, D], BF16) for _ in range(2)]
        ssq = k.sb([128, 32], F32)
        rstd = k.sb([128, 32], F32)
        hT = k.sb([128, 16, MT], BF16)
        wfm = [k.sb([128, 16, 128], BF16) for _ in range(3)]
        wtm = [k.sb([128, 16, 512], BF16) for _ in range(2)]
        stf = [k.sb([128, MT], BF16) for _ in range(2)]
        stt = [k.sb([128, 512], BF16) for _ in range(4)]
        k.LD(g_sb, g_row.partition_broadcast(128), r=[], w=["g_sb"])
        if gate is not None:
            wa1 = [k.sb([128, 16, 16], BF16) for _ in range(2)]
            wa2 = [k.sb([16, 1024], BF16) for _ in range(2)]
            bab = [k.sb([128, 1024], F32) for _ in range(2)]
            t1T = [k.sb([16, MT], BF16) for _ in range(2)]
            pre = [k.sb([128, 1024], F32) for _ in range(2)]
            for d in range(2):
                k.LD(wa1[d], gate["wa1"][d].rearrange("(kc p) r -> p kc r", p=128), r=[], w=[f"wa1{d}"], eng="pool")
                k.LD(wa2[d], gate["wa2"][d], r=[], w=[f"wa2{d}"], eng="pool")
                k.LD(bab[d], gate["ba"][d].partition_broadcast(128), r=[], w=[f"bab{d}"])
        hT_res = [f"hT{t}{h}" for t in range(NTT) for h in "ab"]
        cnt = 0
        wi = 0
        for mt in range(S_len // MT):
            for t in range(NTT):
                b = t % 2
                col = t % 32
                row0 = mt * MT + t * 128
                k.LD(xt[b], x_src[row0:row0 + 128, :], r=[], w=[f"xt{b}"])
                k.ACT(junk, xt[b], AF.Square, r=[f"xt{b}"], w=["junk", f"ssq{col}"], accum_out=ssq[:, col:col + 1])
                k.ACT(rstd[:, col:col + 1], ssq[:, col:col + 1], AF.Sqrt, r=[f"ssq{col}"], w=[f"rstd{col}"],
                      bias=k.eps[:, 0:1], scale=1.0 / D)
                S.dve(lambda e, o=rstd[:, col:col + 1]: e.reciprocal(out=o, in_=o), r=[f"rstd{col}"], w=[f"rstd{col}"])
                S.dve(lambda e, o=hb[b], i0=xt[b], sc=rstd[:, col:col + 1]: e.scalar_tensor_tensor(
                    out=o, in0=i0, scalar=sc, in1=g_sb, op0=ALU.mult, op1=ALU.mult),
                    r=[f"xt{b}", f"rstd{col}", "g_sb"], w=[f"hb{b}"])
                for kc in range(16):
                    k.TR(k.psb(kc // 8)[:, (kc % 8) * 128:(kc % 8 + 1) * 128], hb[b][:, kc * 128:(kc + 1) * 128],
                         r=[f"hb{b}"], w=[f"ptr{kc // 8}"])
                S.act(lambda e, o=hT[:, 0:8, t * 128:(t + 1) * 128], i=k.psb(0).rearrange("p (a b) -> p a b", a=8):
                      e.copy(out=o, in_=i), r=["ptr0"], w=[f"hT{t}a"])
                S.dve(lambda e, o=hT[:, 8:16, t * 128:(t + 1) * 128], i=k.psb(1).rearrange("p (a b) -> p a b", a=8):
                      e.tensor_copy(out=o, in_=i), r=["ptr1"], w=[f"hT{t}b"])
            for (w_ap, dst) in fm:
                n = w_ap.shape[1]
                wv = w_ap.rearrange("(kc p) n -> p kc n", p=128)
                for dtile in range(n // 128):
                    wb = wi % 3
                    wi += 1
                    k.LD(wfm[wb], wv[:, :, dtile * 128:(dtile + 1) * 128], r=[], w=[f"wfm{wb}"], eng="pool")
                    sf = cnt % 2
                    for tb in range(NTB):
                        pb = 2 + cnt % 6
                        for kc in range(16):
                            k.MM(k.ps[pb][:], wfm[wb][:, kc, :], hT[:, kc, tb * 512:(tb + 1) * 512], kc == 0, kc == 15,
                                 r=[f"wfm{wb}"] + hT_res[tb * 8:(tb + 1) * 8], w=[f"ps{pb}"])
                        k.evac(cnt, stf[sf][:, tb * 512:(tb + 1) * 512], k.ps[pb][:], r=[f"ps{pb}"], w=[f"stf{sf}_{tb}"])
                        cnt += 1
                    k.ST(dst[dtile * 128:(dtile + 1) * 128, mt * MT:(mt + 1) * MT], stf[sf],
                         r=[f"stf{sf}_{tb}" for tb in range(NTB)], w=[k.name("dfm")])
                    cnt += (NTB % 2 == 0)
            for (w_ap, dst, scale) in tm:
                n = w_ap.shape[1]
                wv = w_ap.rearrange("(kc p) n -> p kc n", p=128)
                for db in range(n // 512):
                    wb = wi % 2
                    wi += 1
                    k.LD(wtm[wb], wv[:, :, db * 512:(db + 1) * 512], r=[], w=[f"wtm{wb}"], eng="pool")
                    for tt in range(NTT):
                        pb = 2 + cnt % 6
                        sb_ = cnt % 4
                        for kc in range(16):
                            k.MM(k.ps[pb][:], hT[:, kc, tt * 128:(tt + 1) * 128], wtm[wb][:, kc, :], kc == 0, kc == 15,
                                 r=[f"wtm{wb}", f"hT{tt}a", f"hT{tt}b"], w=[f"ps{pb}"])
                        k.evac(cnt, stt[sb_], k.ps[pb][:], r=[f"ps{pb}"], w=[f"stt{sb_}"], scale=scale)
                        row0 = mt * MT + tt * 128
                        k.ST(dst[row0:row0 + 128, db * 512:(db + 1) * 512], stt[sb_], r=[f"stt{sb_}"], w=[k.name("dtm")])
                        cnt += 1
            if gate is not None:
                for d in range(2):
                    for tb in range(NTB):
                        pb = 2 + cnt % 6
                        for kc in range(16):
                            k.MM(k.ps[pb][0:16, :], wa1[d][:, kc, :], hT[:, kc, tb * 512:(tb + 1) * 512], kc == 0, kc == 15,
                                 r=[f"wa1{d}"] + hT_res[tb * 8:(tb + 1) * 8], w=[f"ps{pb}"])
                        k.evac(cnt, t1T[d][:, tb * 512:(tb + 1) * 512], k.ps[pb][0:16, :], r=[f"ps{pb}"], w=[f"t1T{d}_{tb}"])
                        cnt += 1
                for tt in range(NTT):
                    for d in range(2):
                        for hf in range(2):
                            pb = 2 + cnt % 6
                            k.MM(k.ps[pb][:], t1T[d][:, tt * 128:(tt + 1) * 128], wa2[d][:, hf * 512:(hf + 1) * 512], True, True,
                                 r=[f"t1T{d}_{tt // 4}", f"wa2{d}"], w=[f"ps{pb}"])
                            S.dve(lambda e, o=pre[d][:, hf * 512:(hf + 1) * 512], i0=k.ps[pb][:], i1=bab[d][:, hf * 512:(hf + 1) * 512]:
                                  e.tensor_tensor(out=o, in0=i0, in1=i1, op=ALU.add),
                                  r=[f"ps{pb}", f"bab{d}"], w=[f"pre{d}_{hf}"])
                            cnt += 1
                        k.ACT(pre[d], pre[d], AF.Exp, r=[f"pre{d}_0", f"pre{d}_1"], w=[f"pre{d}_0", f"pre{d}_1"], scale=-1.0)
                        k.ACT(pre[d], pre[d], AF.Ln, r=[f"pre{d}_0", f"pre{d}_1"], w=[f"pre{d}_0", f"pre{d}_1"],
                              bias=k.one[:, 0:1], scale=1.0)
                        row0 = mt * MT + tt * 128
                        k.ST(gate["dst"][d][row0:row0 + 128, :], pre[d], r=[f"pre{d}_0", f"pre{d}_1"], w=[k.name("dL")])

    def phaseC(self, x_src, x_dst, S_len, m_ap, z_ap, wout_ap, gpost_row, gnorm_row=None):
        k = self
        S = self.S
        k.phase()
        wo = k.sb([128, 32, D], BF16)
        xt = k.sb([128, D], F32)
        mt_ = k.sb([128, E], BF16)
        zt = k.sb([128, E], BF16)
        gT = k.sb([128, 32, 128], BF16)
        tmp = k.sb([128, D], F32)
        gp = k.sb([128, D], F32)
        junk = k.sb([128, D], BF16)
        st = k.sb([128, 16], F32)
        k.LD(gp, gpost_row.partition_broadcast(128), r=[], w=["gp"])
        if gnorm_row is not None:
            gn = k.sb([128, 1024], F32)
            k.LD(gn, gnorm_row.partition_broadcast(128), r=[], w=["gn"])
        wv = wout_ap.rearrange("(kc p) n -> p kc n", p=128)
        for c in range(8):
            k.LD(wo[:, c * 4:(c + 1) * 4, :], wv[:, c * 4:(c + 1) * 4, :], r=[], w=[f"wo{c}"], eng="pool")
        wo_res = [f"wo{c}" for c in range(8)]
        for t in range(S_len // 128):
            r0 = t * 128
            k.LD(mt_, m_ap[r0:r0 + 128, :], r=[], w=["mt"])
            k.LD(zt, z_ap[r0:r0 + 128, :], r=[], w=["zt"])
            k.LD(xt, x_src[r0:r0 + 128, :], r=[], w=["xt"])
            k.ACT(zt, zt, AF.Silu, r=["zt"], w=["zt"])
            if gnorm_row is not None:
                for h in range(4):
                    k.ACT(junk[:, 0:1024], mt_[:, h * 1024:(h + 1) * 1024], AF.Square, r=["mt"], w=["junk", f"gs{h}"],
                          accum_out=st[:, h:h + 1])
                    k.ACT(st[:, h:h + 1], st[:, h:h + 1], AF.Sqrt, r=[f"gs{h}"], w=[f"gs{h}"], bias=k.eps[:, 0:1], scale=1.0 / 1024)
                    S.dve(lambda e, o=st[:, h:h + 1]: e.reciprocal(out=o, in_=o), r=[f"gs{h}"], w=[f"gs{h}"])
                    S.dve(lambda e, o=mt_[:, h * 1024:(h + 1) * 1024], sc=st[:, h:h + 1]: e.scalar_tensor_tensor(
                        out=o, in0=o, scalar=sc, in1=gn, op0=ALU.mult, op1=ALU.mult), r=["mt", f"gs{h}", "gn"], w=["mt"])
            S.dve(lambda e: e.tensor_tensor(out=mt_, in0=mt_, in1=zt, op=ALU.mult), r=["mt", "zt"], w=["mt"])
            for kc in range(32):
                k.TR(k.psb(kc // 8)[:, (kc % 8) * 128:(kc % 8 + 1) * 128], mt_[:, kc * 128:(kc + 1) * 128],
                     r=["mt"], w=[f"ptr{kc // 8}"])
            for q in range(4):
                src = k.psb(q).rearrange("p (a b) -> p a b", a=8)
                dst = gT[:, q * 8:(q + 1) * 8, :]
                if q % 2 == 0:
                    S.act(lambda e, o=dst, i=src: e.copy(out=o, in_=i), r=[f"ptr{q}"], w=[f"gT{q}"])
                else:
                    S.dve(lambda e, o=dst, i=src: e.tensor_copy(out=o, in_=i), r=[f"ptr{q}"], w=[f"gT{q}"])
            for kc in range(32):
                for nb in range(4):
                    k.MM(k.ps[4 + nb][:], gT[:, kc, :], wo[:, kc, nb * 512:(nb + 1) * 512], kc == 0, kc == 31,
                         r=[f"gT{kc // 8}", wo_res[kc // 4]], w=[f"py{nb}"])
            for nb in range(4):
                k.ACT(junk[:, nb * 512:(nb + 1) * 512], k.ps[4 + nb][:], AF.Square, r=[f"py{nb}"], w=[f"junk{nb}", f"ys{nb}"],
                      accum_out=st[:, 4 + nb:5 + nb])
            S.dve(lambda e: e.tensor_reduce(out=st[:, 8:9], in_=st[:, 4:8], axis=AX.X, op=ALU.add),
                  r=[f"ys{nb}" for nb in range(4)], w=["ysum"])
            k.ACT(st[:, 8:9], st[:, 8:9], AF.Sqrt, r=["ysum"], w=["ysum"], bias=k.eps[:, 0:1], scale=1.0 / D)
            S.dve(lambda e: e.reciprocal(out=st[:, 8:9], in_=st[:, 8:9]), r=["ysum"], w=["ysum"])
            for nb in range(4):
                S.dve(lambda e, o=tmp[:, nb * 512:(nb + 1) * 512], i0=k.ps[4 + nb][:], i1=gp[:, nb * 512:(nb + 1) * 512]:
                      e.scalar_tensor_tensor(out=o, in0=i0, scalar=st[:, 8:9], in1=i1, op0=ALU.mult, op1=ALU.mult),
                      r=[f"py{nb}", "ysum", "gp"], w=[f"tmp{nb}"])
            S.dve(lambda e: e.tensor_tensor(out=tmp, in0=tmp, in1=xt, op=ALU.add),
                  r=[f"tmp{nb}" for nb in range(4)] + ["xt"], w=[f"tmp{nb}" for nb in range(4)])
            k.ST(x_dst[r0:r0 + 128, :], tmp, r=[f"tmp{nb}" for nb in range(4)], w=[k.name("xo")])

    def fnet(self, S_len, uT, Zs, Bs, m_out, c_fc, c_m1, c_m2):
        k = self
        S = self.S
        R = S_len // 128
        k.phase()
        fc = k.sb([128, 4, 1024], BF16)
        k.LD(fc, c_fc, r=[], w=["fc"])
        TM_ = min(512, S_len)
        ut = [k.sb([128, 32, TM_], BF16) for _ in range(2)]
        zst = [k.sb([128, 1024], BF16) for _ in range(4)]
        uv = uT.rearrange("(c p) s -> p c s", p=128)
        cnt = 0
        for mtile in range(S_len // TM_):
            b = mtile % 2
            k.LD(ut[b], uv[:, :, mtile * TM_:(mtile + 1) * TM_], r=[], w=[f"ut{b}"])
            for tt in range(TM_ // 128):
                row0 = mtile * TM_ + tt * 128
                for g in range(8):
                    pb = (cnt % 4) * 2
                    zb = cnt % 4
                    for hf in range(2):
                        for kc in range(4):
                            k.MM(k.ps[pb + hf][:], ut[b][:, g * 4 + kc, tt * 128:(tt + 1) * 128], fc[:, kc, hf * 512:(hf + 1) * 512],
                                 kc == 0, kc == 3, r=[f"ut{b}", "fc"], w=[f"ps{pb + hf}"])
                        k.evac(cnt + hf, zst[zb][:, hf * 512:(hf + 1) * 512], k.ps[pb + hf][:], r=[f"ps{pb + hf}"], w=[f"zst{zb}_{hf}"])
                    k.ST(Zs[row0:row0 + 128, g * 1024:(g + 1) * 1024], zst[zb], r=[f"zst{zb}_0", f"zst{zb}_1"], w=[k.name("Z")])
                    cnt += 1
        k.phase()
        zt = [k.sb([R, 8192], BF16) for _ in range(2)]
        m1 = [k.sb([R, 3, R], BF16) for _ in range(2)]
        bst = [k.sb([R, 8192], BF16) for _ in range(2)]
        Zv = Zs.rearrange("(n1 n2) c -> n2 n1 c", n2=128)
        Bv = Bs.rearrange("(k1 n2) c -> n2 k1 c", n2=128)
        cnt = 0
        for n2 in range(128):
            b = n2 % 2
            k.LD(zt[b], Zv[n2], r=[], w=[f"zt{b}"])
            k.LD(m1[b], c_m1[:, n2, :, :], r=[], w=[f"m1{b}"])
            for g in range(8):
                pb = (cnt % 4) * 2
                zr = zt[b][:, g * 1024:g * 1024 + 512]
                zi = zt[b][:, g * 1024 + 512:(g + 1) * 1024]
                C_, S_, nS_ = m1[b][:, 0, :], m1[b][:, 1, :], m1[b][:, 2, :]
                k.MM(k.ps[pb][0:R, :], C_, zr, True, False, r=[f"zt{b}", f"m1{b}"], w=[f"ps{pb}"])
                k.MM(k.ps[pb][0:R, :], S_, zi, False, True, r=[f"zt{b}", f"m1{b}"], w=[f"ps{pb}"])
                k.MM(k.ps[pb + 1][0:R, :], C_, zi, True, False, r=[f"zt{b}", f"m1{b}"], w=[f"ps{pb + 1}"])
                k.MM(k.ps[pb + 1][0:R, :], nS_, zr, False, True, r=[f"zt{b}", f"m1{b}"], w=[f"ps{pb + 1}"])
                k.evac(cnt, bst[b][:, g * 1024:g * 1024 + 512], k.ps[pb][0:R, :], r=[f"ps{pb}"], w=[f"bst{b}_{g}a"])
                k.evac(cnt + 1, bst[b][:, g * 1024 + 512:(g + 1) * 1024], k.ps[pb + 1][0:R, :], r=[f"ps{pb + 1}"], w=[f"bst{b}_{g}b"])
                cnt += 1
            k.ST(Bv[n2], bst[b], r=[f"bst{b}_{g}{h}" for g in range(8) for h in "ab"], w=[k.name("B")])
        k.phase()
        m2 = k.sb([128, 2, 128], BF16)
        k.LD(m2, c_m2, r=[], w=["m2"])
        bt = [k.sb([128, 8192], BF16) for _ in range(2)]
        ost = [k.sb([128, E], BF16) for _ in range(2)]
        Mv = m_out.rearrange("(k2 r) c -> r k2 c", r=R)
        cnt = 0
        for k1 in range(R):
            b = k1 % 2
            k.LD(bt[b], Bs[k1 * 128:(k1 + 1) * 128, :], r=[], w=[f"bt{b}"])
            for g in range(8):
                pb = cnt % 8
                k.MM(k.ps[pb][:], m2[:, 0, :], bt[b][:, g * 1024:g * 1024 + 512], True, False, r=[f"bt{b}", "m2"], w=[f"ps{pb}"])
                k.MM(k.ps[pb][:], m2[:, 1, :], bt[b][:, g * 1024 + 512:(g + 1) * 1024], False, True, r=[f"bt{b}", "m2"], w=[f"ps{pb}"])
                k.evac(cnt, ost[b][:, g * 512:(g + 1) * 512], k.ps[pb][:], r=[f"ps{pb}"], w=[f"ost{b}_{g}"])
                cnt += 1
            k.ST(Mv[k1], ost[b], r=[f"ost{b}_{g}" for g in range(8)], w=[k.name("M")])

    def nat(self, S_len, qT, kT, v, m_out, c_bias):
        k = self
        S = self.S
        rows = S_len // GRID_W
        k.phase()
        bias_int = k.sb([128, 32, 576], BF16)
        bias_edge = k.sb([128, 32, 576], BF16)
        k.LD(bias_int, c_bias[2], r=[], w=["bias_int"])
        qt = k.sb([128, 32, 128], BF16)
        kt = k.sb([128, 32, 576], BF16)
        vt = k.sb([128, 5, E], BF16)
        ssb = [k.sb([128, 576], F32) for _ in range(2)]
        pb_ = [k.sb([128, 640], BF16) for _ in range(2)]
        pT = [k.sb([128, 5, 128], BF16) for _ in range(2)]
        ost = k.sb([128, E], BF16)
        stat = k.sb([128, 8], F32)
        qv = qT.rearrange("(h p) s -> p h s", p=128)
        kv = kT.rearrange("(h p) s -> p h s", p=128)
        scale = 128.0 ** -0.5
        cnt = 0
        for i in range(rows // 2):
            kb = min(max(2 * i - 4, 0), rows - 9)
            d = 2 * i - kb
            if d == 4:
                bias = bias_int
                bres = "bias_int"
            else:
                k.LD(bias_edge, c_bias[NAT_PIDX[d]], r=[], w=["bias_edge"])
                bias = bias_edge
                bres = "bias_edge"
            k.LD(qt, qv[:, :, i * 128:(i + 1) * 128], r=[], w=["qt"])
            k.LD(kt, kv[:, :, kb * 64:kb * 64 + 576], r=[], w=["kt"])
            for c in range(4):
                k.LD(vt[:, c, :], v[kb * 64 + c * 128:kb * 64 + (c + 1) * 128, :], r=[], w=[f"vt{c}"])
            k.LD(vt[0:64, 4, :], v[kb * 64 + 512:kb * 64 + 576, :], r=[], w=["vt4"])
            for h in range(32):
                b = cnt % 2
                p0 = 0 if b == 0 else 2
                ptb = 4 + b
                pob = 6 + b
                k.MM(k.ps[p0][:], qt[:, h, :], kt[:, h, 0:512], True, True, r=["qt", "kt"], w=[f"ps{p0}"])
                k.MM(k.ps[p0 + 1][:, 0:64], qt[:, h, :], kt[:, h, 512:576], True, True, r=["qt", "kt"], w=[f"ps{p0 + 1}"])
                if NATDBG < 2:
                    cnt += 1
                    continue
                S.dve(lambda e, o=ssb[b][:, 0:512], i0=k.ps[p0][:], i1=bias[:, h, 0:512]:
                      e.scalar_tensor_tensor(out=o, in0=i0, scalar=scale, in1=i1, op0=ALU.mult, op1=ALU.add),
                      r=[f"ps{p0}", bres], w=[f"ssb{b}a"])
                S.dve(lambda e, o=ssb[b][:, 512:576], i0=k.ps[p0 + 1][:, 0:64], i1=bias[:, h, 512:576]:
                      e.scalar_tensor_tensor(out=o, in0=i0, scalar=scale, in1=i1, op0=ALU.mult, op1=ALU.add),
                      r=[f"ps{p0 + 1}", bres], w=[f"ssb{b}b"])
                if NATDBG < 3:
                    cnt += 1
                    continue
                S.dve(lambda e, o=stat[:, b:b + 1], i=ssb[b]: e.tensor_reduce(out=o, in_=i, axis=AX.X, op=ALU.max),
                      r=[f"ssb{b}a", f"ssb{b}b"], w=[f"mx{b}"])
                S.dve(lambda e, o=stat[:, b:b + 1]: e.tensor_scalar_mul(out=o, in0=o, scalar1=-1.0), r=[f"mx{b}"], w=[f"mx{b}"])
                if NATDBG < 4:
                    cnt += 1
                    continue
                k.ACT(pb_[b][:, 0:576], ssb[b], AF.Exp, r=[f"ssb{b}a", f"ssb{b}b", f"mx{b}"], w=[f"pb{b}", f"rs{b}"],
                      bias=stat[:, b:b + 1], scale=1.0, accum_out=stat[:, 2 + b:3 + b])
                S.dve(lambda e, o=stat[:, 2 + b:3 + b]: e.reciprocal(out=o, in_=o), r=[f"rs{b}"], w=[f"rs{b}"])
                if NATDBG < 5:
                    cnt += 1
                    continue
                for c in range(5):
                    if c < 4:
                        k.TR(k.psb(ptb)[:, c * 128:(c + 1) * 128], pb_[b][:, c * 128:(c + 1) * 128], r=[f"pb{b}"], w=[f"ps{ptb}"])
                    else:
                        k.TR(k.psb(ptb)[0:64, c * 128:(c + 1) * 128], pb_[b][:, 512:576], r=[f"pb{b}"], w=[f"ps{ptb}"])
                if NATDBG < 6:
                    cnt += 1
                    continue
                src4 = k.psb(ptb)[:, 0:512].rearrange("p (a b) -> p a b", a=4)
                if cnt % 2 == 0:
                    S.act(lambda e, o=pT[b][:, 0:4, :], i=src4: e.copy(out=o, in_=i), r=[f"ps{ptb}"], w=[f"pT{b}a"])
                    S.act(lambda e, o=pT[b][0:64, 4, :], i=k.psb(ptb)[0:64, 512:640]: e.copy(out=o, in_=i), r=[f"ps{ptb}"], w=[f"pT{b}b"])
                else:
                    S.dve(lambda e, o=pT[b][:, 0:4, :], i=src4: e.tensor_copy(out=o, in_=i), r=[f"ps{ptb}"], w=[f"pT{b}a"])
                    S.dve(lambda e, o=pT[b][0:64, 4, :], i=k.psb(ptb)[0:64, 512:640]: e.tensor_copy(out=o, in_=i), r=[f"ps{ptb}"], w=[f"pT{b}b"])
                if NATDBG < 7:
                    cnt += 1
                    continue
                for c in range(5):
                    if c < 4:
                        k.MM(k.ps[pob][:, 0:128], pT[b][:, c, :], vt[:, c, h * 128:(h + 1) * 128], c == 0, False,
                             r=[f"pT{b}a", f"vt{c}"], w=[f"ps{pob}"])
                    else:
                        k.MM(k.ps[pob][:, 0:128], pT[b][0:64, 4, :], vt[0:64, 4, h * 128:(h + 1) * 128], False, True,
                             r=[f"pT{b}b", "vt4"], w=[f"ps{pob}"])
                if NATDBG < 8:
                    cnt += 1
                    continue
                if cnt % 2 == 0:
                    k.ACT(ost[:, h * 128:(h + 1) * 128], k.ps[pob][:, 0:128], AF.Copy, r=[f"ps{pob}", f"rs{b}"], w=[f"ost{h}"],
                          scale=stat[:, 2 + b:3 + b])
                else:
                    S.dve(lambda e, o=ost[:, h * 128:(h + 1) * 128], i=k.ps[pob][:, 0:128], sc=stat[:, 2 + b:3 + b]:
                          e.tensor_scalar_mul(out=o, in0=i, scalar1=sc), r=[f"ps{pob}", f"rs{b}"], w=[f"ost{h}"])
                cnt += 1
            k.ST(m_out[i * 128:(i + 1) * 128, :], ost, r=[f"ost{h}" for h in range(32)], w=[k.name("mo")])

    def gla(self, S_len, q, kk, v, Lf, Lb, of_s, m_out, c_mats, c_masks):
        k = self
        S = self.S
        n = S_len // 128
        k.phase()
        mats = k.sb([128, 6, 128], F32)
        masks = k.sb([128, 2, 128], F32)
        onec = k.sb([128, 1], F32)
        k.LD(mats, c_mats, r=[], w=["mats"])
        k.LD(masks, c_masks, r=[], w=["masks"])
        S.dve(lambda e: e.memset(onec, -1.0 / 16.0), r=[], w=["onec"])
        qt = k.sb([128, 1024], BF16)
        ktl = k.sb([128, 1024], BF16)
        vt = k.sb([128, E], BF16)
        Lt = k.sb([128, 1024], F32)
        Ex = [k.sb([128, 1024], F32) for _ in range(4)]
        prod = [k.sb([128, 1024], BF16) for _ in range(4)]
        Tt = k.sb([128, 24, 128], BF16)
        sT = [k.sb([128, 128], BF16) for _ in range(2)]
        dec = k.sb([128, 8], F32)
        st32 = k.sb([128, 8, 1024], F32)
        st16 = k.sb([128, 8, 1024], BF16)
        ost = k.sb([128, E], BF16)
        oft = k.sb([128, E], BF16)
        for di in range(2):
            Ld = Lf if di == 0 else Lb
            S.dve(lambda e: e.memset(st32, 0.0), r=[f"st32_{j}" for j in range(8)], w=[f"st32_{j}" for j in range(8)])
            S.dve(lambda e: e.memset(st16, 0.0), r=[f"st16_{j}" for j in range(8)], w=[f"st16_{j}" for j in range(8)])
            order = range(n) if di == 0 else range(n - 1, -1, -1)
            for c in order:
                r0 = c * 128
                k.LD(qt, q[r0:r0 + 128, :], r=[], w=["qt"])
                k.LD(ktl, kk[r0:r0 + 128, :], r=[], w=["ktl"])
                k.LD(vt, v[r0:r0 + 128, :], r=[], w=["vt"])
                k.LD(Lt, Ld[r0:r0 + 128, :], r=[], w=["Lt"])
                if di == 1:
                    k.LD(oft, of_s[r0:r0 + 128, :], r=[f"ofd{c}"], w=["oft"])
                for gi, (mi, outs) in enumerate([(0, [(0, 1.0), (1, -1.0)]), (1, [(2, 1.0)]), (2, [(3, 1.0)])]):
                    for hf in range(2):
                        k.MM(k.ps[hf][:], mats[:, di * 3 + mi, :], Lt[:, hf * 512:(hf + 1) * 512], True, True,
                             r=["mats", "Lt"], w=[f"ps{hf}"])
                    for (ei, sgn) in outs:
                        for hf in range(2):
                            k.ACT(Ex[ei][:, hf * 512:(hf + 1) * 512], k.ps[hf][:], AF.Exp, r=[f"ps{hf}"], w=[f"Ex{ei}_{hf}"], scale=sgn)
                for j in range(8):
                    k.MM(k.ps[2][:, j:j + 1], Lt[:, j * 128:(j + 1) * 128], onec, True, True, r=["Lt", "onec"], w=["ps2"])
                k.ACT(dec, k.ps[2][:, 0:8], AF.Exp, r=["ps2"], w=["dec"])
                for pi, (src, ei, sres) in enumerate([(qt, 0, "qt"), (ktl, 1, "ktl"), (qt, 2, "qt"), (ktl, 3, "ktl")]):
                    S.dve(lambda e, o=prod[pi], a=src, b_=Ex[ei]: e.tensor_tensor(out=o, in0=a, in1=b_, op=ALU.mult),
                          r=[sres, f"Ex{ei}_0", f"Ex{ei}_1"], w=[f"prod{pi}"])
                for ti, pi in enumerate([0, 1, 2]):
                    for j in range(8):
                        k.TR(k.psb(3)[:, j * 128:(j + 1) * 128], prod[pi][:, j * 128:(j + 1) * 128], r=[f"prod{pi}"], w=["ps3"])
                    srcv = k.psb(3).rearrange("p (a b) -> p a b", a=8)
                    if ti % 2 == 0:
                        S.act(lambda e, o=Tt[:, ti * 8:(ti + 1) * 8, :], i=srcv: e.copy(out=o, in_=i), r=["ps3"], w=[f"Tt{ti}"])
                    else:
                        S.dve(lambda e, o=Tt[:, ti * 8:(ti + 1) * 8, :], i=srcv: e.tensor_copy(out=o, in_=i), r=["ps3"], w=[f"Tt{ti}"])
                for h in range(4):
                    sb_ = h % 2
                    for kt_ in range(2):
                        k.MM(k.ps[4][:, 0:128], Tt[:, 8 + h * 2 + kt_, :], Tt[:, h * 2 + kt_, :], kt_ == 0, kt_ == 1,
                             r=["Tt0", "Tt1"], w=["ps4"])
                    S.dve(lambda e, o=sT[sb_], i0=k.ps[4][:, 0:128], i1=masks[:, di, :]: e.tensor_tensor(out=o, in0=i0, in1=i1, op=ALU.mult),
                          r=["ps4", "masks"], w=[f"sT{sb_}"])
                    for blk in range(2):
                        pob = 5 + blk
                        vsl = vt[:, h * 1024 + blk * 512:h * 1024 + (blk + 1) * 512]
                        k.MM(k.ps[pob][:], sT[sb_], vsl, True, False, r=[f"sT{sb_}", "vt"], w=[f"ps{pob}"])
                        for kt_ in range(2):
                            k.MM(k.ps[pob][:], Tt[:, 16 + h * 2 + kt_, :], st16[:, h * 2 + kt_, blk * 512:(blk + 1) * 512], False, kt_ == 1,
                                 r=["Tt2", f"st16_{h * 2 + kt_}"], w=[f"ps{pob}"])
                        osl = ost[:, h * 1024 + blk * 512:h * 1024 + (blk + 1) * 512]
                        if di == 0:
                            k.evac(blk, osl, k.ps[pob][:], r=[f"ps{pob}"], w=[f"ost{h}_{blk}"])
                        else:
                            S.dve(lambda e, o=osl, i0=k.ps[pob][:], i1=oft[:, h * 1024 + blk * 512:h * 1024 + (blk + 1) * 512]:
                                  e.tensor_tensor(out=o, in0=i0, in1=i1, op=ALU.add), r=[f"ps{pob}", "oft"], w=[f"ost{h}_{blk}"])
                    for kt_ in range(2):
                        j = h * 2 + kt_
                        for blk in range(2):
                            k.MM(k.ps[7][:], prod[3][:, j * 128:(j + 1) * 128], vt[:, h * 1024 + blk * 512:h * 1024 + (blk + 1) * 512],
                                 True, True, r=["prod3", "vt"], w=["ps7"])
                            S.dve(lambda e, o=st32[:, j, blk * 512:(blk + 1) * 512], sc=dec[:, j:j + 1], i1=k.ps[7][:]:
                                  e.scalar_tensor_tensor(out=o, in0=o, scalar=sc, in1=i1, op0=ALU.mult, op1=ALU.add),
                                  r=["ps7", "dec", f"st32_{j}"], w=[f"st32_{j}"])
                        S.act(lambda e, o=st16[:, j, :], i=st32[:, j, :]: e.copy(out=o, in_=i), r=[f"st32_{j}"], w=[f"st16_{j}"])
                dst = of_s if di == 0 else m_out
                k.ST(dst[r0:r0 + 128, :], ost, r=[f"ost{h}_{blk}" for h in range(4) for blk in range(2)],
                     w=[f"ofd{c}" if di == 0 else k.name("go")])


def build_program(unit_lens, layers, out_prompt_rows=None, debug=None):
    Smax = max(unit_lens)
    kb = K(None, layers)
    nc = kb.nc
    S = kb.S
    dt_in = {}

    def ext(name, shape, dt):
        dt_in[name] = (shape, dt)
        return nc.dram_tensor(name, shape, dt, kind="ExternalInput").ap()

    x_in = [ext(f"x{u}", [L, D], F32) for u, L in enumerate(unit_lens)]
    npre = ext("norm_pre_g", [4, D], F32)
    npost = ext("norm_post_g", [4, D], F32)
    fnet_w_in = ext("fnet_w_in", [2, D, 2 * E], F32)
    fnet_w_out = ext("fnet_w_out", [2, E, D], F32)
    nat_w_in = ext("nat_w_in", [1, D, 4 * E], F32)
    nat_w_out = ext("nat_w_out", [1, E, D], F32)
    gla_w_in = ext("gla_w_in", [1, D, 2048 + 2 * E], F32)
    gla_w_out = ext("gla_w_out", [1, E, D], F32)
    wa1 = [ext("gla_wa1_f", [1, D, 16], F32), ext("gla_wa1_b", [1, D, 16], F32)]
    wa2 = [ext("gla_wa2_f", [1, 16, 1024], F32), ext("gla_wa2_b", [1, 16, 1024], F32)]
    ba = [ext("gla_ba_f", [1, 1024], F32), ext("gla_ba_b", [1, 1024], F32)]
    gnorm = ext("gla_g_norm", [1, 1024], F32)
    c_ident = ext("c_ident", [128, 128], BF16)
    c_fc = ext("c_fc", [128, 4, 1024], BF16)
    c_m2 = ext("c_m2", [128, 2, 128], BF16)
    c_m1 = {}
    for L in sorted(set(unit_lens)):
        R = L // 128
        c_m1[L] = ext(f"c_m1_{L}", [R, 128, 3, R], BF16)
    c_bias = ext("c_bias", [5, 128, 32, 576], BF16)
    c_mats = ext("c_gmats", [128, 6, 128], F32)
    c_masks = ext("c_gmasks", [128, 2, 128], F32)

    outs = []
    x_st = []
    for u, L in enumerate(unit_lens):
        if u == 0 and out_prompt_rows is not None:
            st_ap = nc.dram_tensor("xp_state", [L, D], F32).ap()
            o = nc.dram_tensor("y0", [out_prompt_rows, D], F32, kind="ExternalOutput").ap()
            outs.append(o)
        else:
            st_ap = nc.dram_tensor(f"y{u}", [L, D], F32, kind="ExternalOutput").ap()
            outs.append(st_ap)
        x_st.append(st_ap)
    scrA = nc.dram_tensor("scrA", [E, Smax], BF16).ap()
    scrB = nc.dram_tensor("scrB", [E, Smax], BF16).ap()
    scrV = nc.dram_tensor("scrV", [Smax, E], BF16).ap()
    scrZ = nc.dram_tensor("scrZ", [Smax, E], BF16).ap()
    scrM = nc.dram_tensor("scrM", [Smax, E], BF16).ap()
    scrO = nc.dram_tensor("scrO", [Smax, E], BF16).ap()
    scrQ = nc.dram_tensor("scrQ", [Smax, 1024], BF16).ap()
    scrK = nc.dram_tensor("scrK", [Smax, 1024], BF16).ap()
    scrL = [nc.dram_tensor("scrLf", [Smax, 1024], F32).ap(), nc.dram_tensor("scrLb", [Smax, 1024], F32).ap()]
    scrZZ = nc.dram_tensor("scrZZ", [Smax, 8192], BF16).ap()
    scrBB = nc.dram_tensor("scrBB", [Smax, 8192], BF16).ap()

    with contextlib.ExitStack() as st:
        kb.setup_mem(st)
        kb.LD(kb.ident, c_ident, r=[], w=["ident"])
        S.dve(lambda e: e.memset(kb.eps, EPS), r=[], w=["eps"])
        S.dve(lambda e: e.memset(kb.one, 1.0), r=[], w=["one"])
        for li, (kind, j) in enumerate(layers if True else []):
          try:
            for u, L in enumerate(unit_lens):
                xs = x_in[u] if li == 0 else x_st[u]
                xd = x_st[u]
                if kind == "fnet":
                    w = fnet_w_in[j]
                    kb.phaseA(xs, L, npre[li:li + 1, :], fm=[(w[:, 0:E], scrA[:, 0:L])], tm=[(w[:, E:2 * E], scrZ[0:L, :], 1.0)])
                    kb.fnet(L, scrA[:, 0:L], scrZZ[0:L, :], scrBB[0:L, :], scrM[0:L, :], c_fc, c_m1[L], c_m2)
                    kb.phaseC(xs, xd, L, scrM[0:L, :], scrZ[0:L, :], fnet_w_out[j], npost[li:li + 1, :])
                elif kind == "nat":
                    w = nat_w_in[j]
                    kb.phaseA(xs, L, npre[li:li + 1, :], fm=[(w[:, 0:E], scrA[:, 0:L]), (w[:, E:2 * E], scrB[:, 0:L])],
                              tm=[(w[:, 2 * E:3 * E], scrV[0:L, :], 1.0), (w[:, 3 * E:4 * E], scrZ[0:L, :], 1.0)])
                    kb.nat(L, scrA[:, 0:L], scrB[:, 0:L], scrV[0:L, :], scrM[0:L, :], c_bias)
                    kb.phaseC(xs, xd, L, scrM[0:L, :], scrZ[0:L, :], nat_w_out[j], npost[li:li + 1, :])
                else:
                    w = gla_w_in[j]
                    gate = dict(wa1=[wa1[0][j], wa1[1][j]], wa2=[wa2[0][j], wa2[1][j]],
                                ba=[ba[0][j:j + 1, :], ba[1][j:j + 1, :]], dst=[scrL[0][0:L, :], scrL[1][0:L, :]])
                    kb.phaseA(xs, L, npre[li:li + 1, :], fm=[],
                              tm=[(w[:, 0:1024], scrQ[0:L, :], 1.0 / 16.0), (w[:, 1024:2048], scrK[0:L, :], 1.0),
                                  (w[:, 2048:2048 + E], scrV[0:L, :], 1.0), (w[:, 2048 + E:2048 + 2 * E], scrZ[0:L, :], 1.0)],
                              gate=gate)
                    kb.gla(L, scrQ[0:L, :], scrK[0:L, :], scrV[0:L, :], scrL[0][0:L, :], scrL[1][0:L, :], scrO[0:L, :], scrM[0:L, :],
                           c_mats, c_masks)
                    kb.phaseC(xs, xd, L, scrM[0:L, :], scrZ[0:L, :], gla_w_out[j], npost[li:li + 1, :], gnorm_row=gnorm[j:j + 1, :])
          except StopBuild:
            break
        if out_prompt_rows is not None:
            kb.phase()
            pidc = {}

            def final_copy(e):
                pid = e.partition_id()
                return e.dma_start(out=outs[0], in_=x_st[0][bass.ds(pid * out_prompt_rows, out_prompt_rows), :])
            S.dma("pool", final_copy, r=[], w=["y0"])
        S.emit(nc, st)
    return nc, dt_in


def host_consts(inputs, unit_lens):
    c = {}
    c["c_ident"] = np.eye(128, dtype=np.float32).astype(NPBF)
    fc = None
    for L in sorted(set(unit_lens)):
        fc, m1, m2 = fnet_consts(L)
        c[f"c_m1_{L}"] = m1
        c["c_m2"] = m2
    c["c_fc"] = fc
    c["c_bias"] = nat_bias_tables(np.asarray(inputs["nat_rpb"])[0])
    mats, masks = gla_consts()
    c["c_gmats"] = mats
    c["c_gmasks"] = masks
    return c


WKEYS = ["norm_pre_g", "norm_post_g", "fnet_w_in", "fnet_w_out", "nat_w_in", "nat_w_out", "gla_w_in", "gla_w_out",
         "gla_wa1_f", "gla_wa1_b", "gla_wa2_f", "gla_wa2_b", "gla_ba_f", "gla_ba_b", "gla_g_norm"]


def kernel(**inputs):
    x_prompt = np.asarray(inputs["x_prompt"], np.float32)
    x_sample = np.asarray(inputs["x_sample"], np.float32)
    unit_lens = [x_prompt.shape[1], x_sample.shape[1], x_sample.shape[1]]
    nc, _ = build_program(unit_lens, LAYERS, out_prompt_rows=x_prompt.shape[1] // 8)
    base = {k_: np.ascontiguousarray(np.asarray(inputs[k_], np.float32)) for k_ in WKEYS}
    base.update(host_consts(inputs, unit_lens))
    in_maps = []
    for c in range(8):
        m = dict(base)
        m["x0"] = np.ascontiguousarray(x_prompt[0])
        m["x1"] = np.ascontiguousarray(x_sample[2 * c])
        m["x2"] = np.ascontiguousarray(x_sample[2 * c + 1])
        in_maps.append(m)
    res = run_bass_kernel_spmd(nc, in_maps, core_ids=list(range(8)))
    yp = np.concatenate([res.results[c]["y0"] for c in range(8)], axis=0)[None]
    ys = np.stack([res.results[c][f"y{u}"] for c in range(8) for u in (1, 2)], axis=0)
    return (yp.astype(np.float32), ys.astype(np.float32))
```
